# Optimizing a Trainium2 kernel written in Bass

```python
import math
import jax, jax.numpy as jnp
from jax import lax
import numpy as np

D_MODEL = 1024
BATCH = 4
SEQ = 8192
DEPTH = 1

MEM_LEN = 256
MIX_WIDTH = D_MODEL
ATT_WIDTH = MIX_WIDTH // 2
CONV_WIDTH = MIX_WIDTH - ATT_WIDTH
ATT_HEADS = 4
ATT_HEAD_DIM = ATT_WIDTH // ATT_HEADS
QK_DIM = ATT_HEAD_DIM // 2
QK_TOTAL = ATT_HEADS * 2 * QK_DIM
CONV_GROUPS = 4
CONV_GROUP_DIM = CONV_WIDTH // CONV_GROUPS
CONV_KERNEL = 31
IN_COLS = 2 * QK_TOTAL + ATT_WIDTH + 2 * CONV_WIDTH
MEM_HEADS = 4
MEM_HEAD_DIM = D_MODEL // MEM_HEADS
D_FF = -(-8 * D_MODEL // (3 * 256)) * 256
Q_BLOCK = 128
LN_EPS = 1e-5
DEEPNORM_ALPHA = (2 * DEPTH) ** 0.25
DEEPNORM_BETA = (8 * DEPTH) ** -0.25

kernel_name = "hybrid_diffattn_conformer_deepnorm"


def layer_norm(x, g, b):
    xf = x.astype(jnp.float32)
    mu = xf.mean(-1, keepdims=True)
    var = jnp.square(xf - mu).mean(-1, keepdims=True)
    return ((xf - mu) * lax.rsqrt(var + LN_EPS) * g.astype(jnp.float32) + b.astype(jnp.float32)).astype(x.dtype)


def rms_norm(x, g):
    xf = x.astype(jnp.float32)
    return xf * lax.rsqrt(jnp.mean(xf * xf, -1, keepdims=True) + LN_EPS) * g.astype(jnp.float32)


def alibi_slopes(n_heads):
    return jnp.exp2(-8.0 * jnp.arange(1, n_heads + 1, dtype=jnp.float32) / n_heads)


def diff_attention(q, k, v, lam, slopes):
    b, s, h, _, dk = q.shape
    nb = s // Q_BLOCK
    qb = q.reshape(b, nb, Q_BLOCK, h, 2, dk).transpose(1, 0, 2, 3, 4, 5)
    vf = v.astype(jnp.float32)
    k_pos = jnp.arange(s)
    scale = dk ** -0.5

    def block(args):
        q_blk, i = args
        q_pos = i * Q_BLOCK + jnp.arange(Q_BLOCK)
        dist = q_pos[:, None] - k_pos[None, :]
        bias = -slopes[:, None, None] * dist.astype(jnp.float32)
        logits = jnp.einsum('bqhcd,bkhcd->bhcqk', q_blk, k,
                            preferred_element_type=jnp.float32) * scale + bias[None, :, None]
        logits = jnp.where(dist[None, None, None] >= 0, logits, -jnp.inf)
        p = jax.nn.softmax(logits, axis=-1)
        a = p[:, :, 0] - lam * p[:, :, 1]
        return jnp.einsum('bhqk,bkhe->bqhe', a, vf)

    out = lax.map(block, (qb, jnp.arange(nb)))
    return out.transpose(1, 0, 2, 3, 4).reshape(b, s, h, -1)


def causal_depthwise_conv(u, w, bias):
    c = u.shape[-1]
    out = lax.conv_general_dilated(u, w[:, None, :].astype(u.dtype), window_strides=(1,),
                                   padding=((w.shape[0] - 1, 0),),
                                   dimension_numbers=('NWC', 'WIO', 'NWC'),
                                   feature_group_count=c)
    return out + bias


def setup_inputs(seed: int = 0) -> dict:
    key = jax.random.key(seed)
    ks = iter(jax.random.split(key, 40))

    def nrm(shape, scale):
        return jax.random.normal(next(ks), shape, jnp.float32) * scale

    def gain(shape):
        return 1.0 + nrm(shape, 0.02)

    L = DEPTH
    d = {}
    d["x"] = nrm((BATCH, SEQ, D_MODEL), 1.0)
    d["mem"] = nrm((BATCH, MEM_LEN, D_MODEL), 1.0)
    d["in_norm_g"] = gain((D_MODEL,))
    d["in_norm_b"] = nrm((D_MODEL,), 0.02)
    d["w_in"] = nrm((L, D_MODEL, IN_COLS), D_MODEL ** -0.5)
    d["lambda_q1"] = nrm((L, QK_DIM), 0.1)
    d["lambda_k1"] = nrm((L, QK_DIM), 0.1)
    d["lambda_q2"] = nrm((L, QK_DIM), 0.1)
    d["lambda_k2"] = nrm((L, QK_DIM), 0.1)
    d["subln_g"] = gain((L, ATT_HEAD_DIM))
    d["conv_w"] = nrm((L, CONV_KERNEL, CONV_WIDTH), CONV_KERNEL ** -0.5)
    d["conv_b"] = nrm((L, CONV_WIDTH), 0.02)
    d["conv_norm_g"] = gain((L, CONV_WIDTH))
    d["conv_norm_b"] = nrm((L, CONV_WIDTH), 0.02)
    d["w_pw"] = nrm((L, CONV_WIDTH, CONV_WIDTH), CONV_WIDTH ** -0.5)
    d["b_pw"] = nrm((L, CONV_WIDTH), 0.02)
    d["w_o"] = nrm((L, MIX_WIDTH, D_MODEL), DEEPNORM_BETA * MIX_WIDTH ** -0.5)
    d["ln1_g"] = gain((L, D_MODEL))
    d["ln1_b"] = nrm((L, D_MODEL), 0.02)
    d["w_q_mem"] = nrm((L, D_MODEL, D_MODEL), D_MODEL ** -0.5)
    d["w_kv_mem"] = nrm((L, D_MODEL, 2 * D_MODEL), D_MODEL ** -0.5)
    d["w_o_mem"] = nrm((L, D_MODEL, D_MODEL), DEEPNORM_BETA * D_MODEL ** -0.5)
    d["ln2_g"] = gain((L, D_MODEL))
    d["ln2_b"] = nrm((L, D_MODEL), 0.02)
    d["w_gate"] = nrm((L, D_MODEL, D_FF), D_MODEL ** -0.5)
    d["w_up"] = nrm((L, D_MODEL, D_FF), D_MODEL ** -0.5)
    d["w_down"] = nrm((L, D_FF, D_MODEL), DEEPNORM_BETA * D_FF ** -0.5)
    d["ln3_g"] = gain((L, D_MODEL))
    d["ln3_b"] = nrm((L, D_MODEL), 0.02)
    return d


def reference(x, mem, in_norm_g, in_norm_b, w_in, lambda_q1, lambda_k1, lambda_q2, lambda_k2,
              subln_g, conv_w, conv_b, conv_norm_g, conv_norm_b, w_pw, b_pw, w_o, ln1_g, ln1_b,
              w_q_mem, w_kv_mem, w_o_mem, ln2_g, ln2_b, w_gate, w_up, w_down, ln3_g, ln3_b):
    bsz, s, _ = x.shape
    slopes = alibi_slopes(ATT_HEADS)
    x = layer_norm(x, in_norm_g, in_norm_b)
    for l in range(DEPTH):
        lam_init = 0.8 - 0.6 * math.exp(-0.3 * l)
        proj = x @ w_in[l]
        q, k, v, c_val, c_gate = jnp.split(
            proj, [QK_TOTAL, 2 * QK_TOTAL, 2 * QK_TOTAL + ATT_WIDTH,
                   2 * QK_TOTAL + ATT_WIDTH + CONV_WIDTH], axis=-1)
        q = q.reshape(bsz, s, ATT_HEADS, 2, QK_DIM)
        k = k.reshape(bsz, s, ATT_HEADS, 2, QK_DIM)
        v = v.reshape(bsz, s, ATT_HEADS, ATT_HEAD_DIM)
        lam = (jnp.exp(jnp.sum(lambda_q1[l] * lambda_k1[l]).astype(jnp.float32))
               - jnp.exp(jnp.sum(lambda_q2[l] * lambda_k2[l]).astype(jnp.float32)) + lam_init)
        att = diff_attention(q, k, v, lam, slopes)
        att = rms_norm(att, subln_g[l]) * (1.0 - lam_init)
        att = att.reshape(bsz, s, ATT_WIDTH).astype(x.dtype)
        glu = c_val * jax.nn.sigmoid(c_gate)
        c = causal_depthwise_conv(glu, conv_w[l], conv_b[l])
        c = layer_norm(c.reshape(bsz, s, CONV_GROUPS, CONV_GROUP_DIM),
                       conv_norm_g[l].reshape(CONV_GROUPS, CONV_GROUP_DIM),
                       conv_norm_b[l].reshape(CONV_GROUPS, CONV_GROUP_DIM)).reshape(bsz, s, CONV_WIDTH)
        c = jax.nn.silu(c) @ w_pw[l] + b_pw[l]
        mix = jnp.concatenate([att, c], axis=-1) @ w_o[l]
        x = layer_norm(DEEPNORM_ALPHA * x + mix, ln1_g[l], ln1_b[l])
        qc = (x @ w_q_mem[l]).reshape(bsz, s, MEM_HEADS, MEM_HEAD_DIM)
        kc, vc = jnp.split(mem @ w_kv_mem[l], 2, axis=-1)
        kc = kc.reshape(bsz, -1, MEM_HEADS, MEM_HEAD_DIM)
        vc = vc.reshape(bsz, -1, MEM_HEADS, MEM_HEAD_DIM)
        logits = jnp.einsum('bshd,bmhd->bhsm', qc, kc,
                            preferred_element_type=jnp.float32) * (MEM_HEAD_DIM ** -0.5)
        p = jax.nn.softmax(logits, axis=-1)
        ca = jnp.einsum('bhsm,bmhd->bshd', p, vc.astype(jnp.float32)).reshape(bsz, s, D_MODEL).astype(x.dtype)
        x = layer_norm(DEEPNORM_ALPHA * x + ca @ w_o_mem[l], ln2_g[l], ln2_b[l])
        hdn = jax.nn.silu(x @ w_gate[l]) * (x @ w_up[l])
        x = layer_norm(DEEPNORM_ALPHA * x + hdn @ w_down[l], ln3_g[l], ln3_b[l])
    return x
```

```python
import math
from contextlib import ExitStack

import numpy as np
import ml_dtypes

import concourse.bass as bass
import concourse.mybir as mybir
from concourse.bass_utils import run_bass_kernel_spmd

F32 = mybir.dt.float32
BF16 = mybir.dt.bfloat16
AF = mybir.ActivationFunctionType
ALU = mybir.AluOpType

PE, ACT, DVE, POOL, SP = "pe", "act", "dve", "pool", "sp"
ENGS = (PE, ACT, DVE, POOL, SP)
EPOCH = 12000

D = 1024
SEQ = 8192
NOWN = 4096
NBLK = 64
NQT = 8
DFF = 2816
NFT = 22
MEM = 256
EPS = 1e-5
ALPHA = 2.0 ** 0.25
LAM_INIT = 0.2
SLOPES = [2.0 ** (-2.0 * (h + 1)) for h in range(4)]
NEG = -30000.0
BT_OFF = 56

C_G0, C_B0, C_G1, C_B1, C_G2, C_B2 = 0, 8, 16, 24, 32, 40
C_CB, C_CNG, C_CNB, C_BPW, C_SUB, C_CW = 48, 52, 56, 60, 64, 65
NCOLP = 192


class Buf:
    __slots__ = ("name", "w", "rs", "excl")

    def __init__(self, name="", excl=False):
        self.name = name
        self.w = None
        self.rs = []
        self.excl = excl


class Op:
    __slots__ = ("eng", "fn", "deps", "need", "dma_key", "semslot", "semval")

    def __init__(self, eng, fn, dma_key):
        self.eng = eng
        self.fn = fn
        self.deps = []
        self.need = False
        self.dma_key = dma_key
        self.semslot = None
        self.semval = None


class Prog:
    def __init__(self, nc):
        self.nc = nc
        self.q = {e: [] for e in ENGS}
        self.dmas = []
        self.stopped = False
        self._cap = None

    def capture_begin(self):
        self._cap = []

    def capture_end(self):
        lst, self._cap = self._cap, None
        return lst

    def replay(self, lst, n):
        while n > 0 and lst:
            a = lst.pop(0)
            self.op(*a)
            n -= 1

    def op(self, eng, fn, reads=(), writes=(), dma_key=None, after=()):
        o = Op(eng, fn, dma_key)
        if self.stopped:
            return o
        if self._cap is not None:
            self._cap.append((eng, fn, tuple(reads), tuple(writes), dma_key, tuple(after)))
            return o
        writes = list(writes) + [b for b in reads if b.excl]
        reads = [b for b in reads if not b.excl]
        deps = list(after)
        for b in reads:
            if b.w is not None:
                deps.append(b.w)
        for b in writes:
            if b.w is not None:
                deps.append(b.w)
            deps.extend(b.rs)
        for b in reads:
            if dma_key is None:
                b.rs = [r for r in b.rs if r.eng != eng or r.dma_key is not None]
            b.rs.append(o)
        for b in writes:
            b.w = o
            b.rs = []
        seen = set()
        for d in deps:
            if d is o or id(d) in seen:
                continue
            seen.add(id(d))
            if d.eng == PE and eng == PE:
                continue
            o.deps.append(d)
            d.need = True
        self.q[eng].append(o)
        if dma_key is not None:
            self.dmas.append(o)
        return o

    def barrier(self):
        lasts = [self.q[e][-1] for e in ENGS if self.q[e]]
        lasts += self.dmas
        self.dmas = []
        for e in ENGS:
            self.op(e, lambda h: h.nop(), after=[l for l in lasts])

    def emit(self, final_ops=()):
        nc = self.nc
        cnt = {}
        for e in ENGS:
            n = 0
            for o in self.q[e]:
                if o.dma_key is not None:
                    key = ("dma", o.dma_key)
                    cnt[key] = cnt.get(key, 0) + 16
                    o.semslot = key
                    o.semval = cnt[key]
                elif o.need:
                    o.semslot = (e, n // EPOCH)
                    o.semval = n % EPOCH + 1
                    n += 1
        keys = []
        ks = set()
        for e in ENGS:
            for o in self.q[e]:
                if o.semslot is not None and o.semslot not in ks:
                    ks.add(o.semslot)
                    keys.append(o.semslot)
        self.nsems = len(keys)
        with ExitStack() as st:
            sems = {}
            for i, k in enumerate(keys):
                sems[k] = st.enter_context(nc.semaphore("s%d" % i))
            block = st.enter_context(nc.Block())

            def run(ename, handle):
                waited = {}
                for o in self.q[ename]:
                    for d in o.deps:
                        k = d.semslot
                        if waited.get(k, 0) >= d.semval:
                            continue
                        handle.wait_ge(sems[k], d.semval)
                        waited[k] = d.semval
                    ins = o.fn(handle)
                    if o.semslot is not None:
                        ins.then_inc(sems[o.semslot], 16 if o.dma_key is not None else 1)
                if ename == SP:
                    lastdma = {}
                    for e2 in ENGS:
                        for o2 in self.q[e2]:
                            if o2.dma_key is not None:
                                lastdma[o2.semslot] = o2
                    for o in list(final_ops) + list(lastdma.values()):
                        if o.semslot is None:
                            continue
                        if waited.get(o.semslot, 0) < o.semval:
                            handle.wait_ge(sems[o.semslot], o.semval)
                            waited[o.semslot] = o.semval

            @block.tensor
            def _(h):
                run(PE, h)

            @block.scalar
            def _(h):
                run(ACT, h)

            @block.vector
            def _(h):
                run(DVE, h)

            @block.gpsimd
            def _(h):
                run(POOL, h)

            @block.sync
            def _(h):
                run(SP, h)


class Arena:
    def __init__(self, U, nbytes):
        self.U = U
        self.nbytes = nbytes
        self.off = 0

    def mark(self):
        return self.off

    def reset(self, m):
        self.off = m

    def alloc(self, shape, dt):
        n = 1
        for s in shape[1:]:
            n *= s
        esz = 2 if dt == BF16 else 4
        size = (n * esz + 63) // 64 * 64
        assert self.off + size <= self.nbytes, ("SBUF arena overflow", self.off, size, self.nbytes)
        v = self.U[0:shape[0], self.off // 2:self.off // 2 + (n * esz) // 2]
        self.off += size
        if dt == F32:
            v = v.bitcast(F32)
        if len(shape) == 3:
            v = v.rearrange("p (a b) -> p a b", a=shape[1])
        elif len(shape) == 4:
            v = v.rearrange("p (a b c) -> p a b c", a=shape[1], b=shape[2])
        return v


class _Stop(Exception):
    pass


def build_program(stop=None):
    nc = bass.Bass("TRN2", target_bir_lowering=False)
    P = Prog(nc)

    def ckp(name):
        if stop == name:
            P.stopped = True

    def din(name, shape, dt=F32):
        return nc.dram_tensor(name, list(shape), dt, kind="ExternalInput").ap()

    xkv = din("xkv", [SEQ, D])
    xown = din("xown", [NOWN, D])
    xhalo = din("xhalo", [1024, D])
    memd = din("mem", [MEM, D])
    w_in = din("w_in", [D, 2560])
    w_pw = din("w_pw", [512, 512])
    w_o = din("w_o", [D, D])
    w_q_mem = din("w_q_mem", [D, D])
    w_kv_mem = din("w_kv_mem", [D, 2 * D])
    w_o_mem = din("w_o_mem", [D, D])
    w_gate = din("w_gate", [D, DFF])
    w_up = din("w_up", [D, DFF])
    w_down = din("w_down", [DFF, D])
    colp_d = din("colp", [128, NCOLP])
    rowp_d = din("rowp", [8, D])
    lamv_d = din("lamv", [4, 64])
    ident_d = din("ident", [128, 128], BF16)
    masks_d = din("masks", [128, 256])
    kbrows_d = din("kbrows", [3, 128], BF16)
    qbrows_d = din("qbrows", [3, 4 * 512], BF16)
    btab_d = din("btab", [128, 256])
    hmask_d = din("hmask", [128, 1])
    y = nc.dram_tensor("y", [NOWN, D], F32, kind="ExternalOutput").ap()

    final_ops = []

    with ExitStack() as st:
        ARENA_BYTES = 206000
        U = st.enter_context(nc.sbuf_tensor("U", [128, ARENA_BYTES // 2], BF16))
        AR = Arena(U, ARENA_BYTES)
        PSt = [st.enter_context(nc.psum_tensor("PS%d" % i, [128, 1024], F32)) for i in range(4)]
        REG = []
        REGB = []
        for i in range(4):
            for hlf in range(2):
                REG.append(PSt[i][:, hlf * 512:(hlf + 1) * 512])
                REGB.append(Buf("bank%d" % (2 * i + hlf), excl=True))

        def bank_bf(bk):
            return REG[bk].bitcast(BF16)

        colp = AR.alloc([128, NCOLP], F32)
        ident = AR.alloc([128, 128], BF16)
        masks = AR.alloc([128, 256], F32)
        btab = AR.alloc([128, 256], F32)
        hmask = AR.alloc([128, 1], F32)
        lamv = AR.alloc([128, 4, 64], F32)
        sc = AR.alloc([128, 16], F32)
        lamt = AR.alloc([128, 2, 64], F32)
        ones_bf = AR.alloc([128, 128], BF16)
        onesm_bf = AR.alloc([128, 128], BF16)
        attT = AR.alloc([128, 4, NOWN], BF16)
        R = AR.alloc([128, 4, D], F32)
        zb = AR.alloc([128, 5, D], BF16)
        xT = AR.alloc([128, 8, 640], BF16)
        stt = AR.alloc([128, 5, 2, 6], F32)
        mv = AR.alloc([128, 5, 2], F32)
        lnt = AR.alloc([128, 5, 4], F32)
        b_colp, b_ident, b_masks, b_btab, b_hmask, b_lamv, b_sc = [Buf() for _ in range(7)]
        b_ones = Buf()
        b_attT = [Buf() for _ in range(4)]
        b_R = [Buf() for _ in range(4)]
        b_zb = [Buf() for _ in range(5)]
        b_xT = [Buf() for _ in range(8)]
        b_st = [Buf() for _ in range(5)]
        b_mv = Buf()
        b_lnt = Buf()
        pmark = AR.mark()

        def dma(eng, out, in_, key, reads=(), writes=()):
            return P.op(eng, lambda e: e.dma_start(out=out, in_=in_), reads=reads, writes=writes, dma_key=key)

        dma(SP, colp, colp_d, "c0", writes=[b_colp])
        dma(SP, ident, ident_d, "c1", writes=[b_ident])
        dma(SP, masks, masks_d, "c2", writes=[b_masks])
        dma(SP, btab, btab_d, "c3", writes=[b_btab])
        dma(SP, hmask, hmask_d, "c4", writes=[b_hmask])
        dma(SP, lamv, bass.AP(lamv_d.tensor, 0, [[0, 128], [64, 4], [1, 64]]), "c5", writes=[b_lamv])
        P.op(DVE, lambda e: e.memset(sc[:, 0:1], EPS), writes=[b_sc])
        P.op(DVE, lambda e: e.memset(ones_bf, 1.0), writes=[b_ones])
        P.op(DVE, lambda e: e.memset(onesm_bf, 1.0 / 128.0), writes=[b_ones])
        b_lamt = Buf()
        P.op(DVE, lambda e: e.tensor_tensor(out=lamt[:, 0, :], in0=lamv[:, 0, :], in1=lamv[:, 1, :], op=ALU.mult),
             reads=[b_lamv], writes=[b_lamt])
        P.op(DVE, lambda e: e.tensor_tensor(out=lamt[:, 1, :], in0=lamv[:, 2, :], in1=lamv[:, 3, :], op=ALU.mult),
             reads=[b_lamv], writes=[b_lamt])
        P.op(DVE, lambda e: e.reduce_sum(out=sc[:, 1:3], in_=lamt, axis=mybir.AxisListType.X),
             reads=[b_lamt], writes=[b_sc])
        P.op(ACT, lambda e: e.activation(out=sc[:, 3:5], in_=sc[:, 1:3], func=AF.Exp), reads=[b_sc], writes=[b_sc])
        P.op(DVE, lambda e: e.tensor_tensor(out=sc[:, 5:6], in0=sc[:, 4:5], in1=sc[:, 3:4], op=ALU.subtract),
             reads=[b_sc], writes=[b_sc])
        P.op(DVE, lambda e: e.tensor_scalar(out=sc[:, 5:6], in0=sc[:, 5:6], scalar1=-LAM_INIT, scalar2=None,
                                            op0=ALU.add), reads=[b_sc], writes=[b_sc])
        P.op(DVE, lambda e: e.tensor_scalar(out=sc[:, 6:7], in0=colp[:, C_SUB:C_SUB + 1], scalar1=1.0 - LAM_INIT,
                                            scalar2=None, op0=ALU.mult), reads=[b_colp, b_sc], writes=[b_sc])

        ckp('const')
        def load_x(src_rows, nblk, j0=0):
            dma(SP, R[:, j0:j0 + nblk, :], src_rows.rearrange("(j p) d -> p j d", p=128), "xR",
                writes=[b_R[j] for j in range(j0, j0 + nblk)])

        def ln_stats(srcs, nb, part="all"):
            srcs = [(a_, list(b_) if isinstance(b_, (tuple, list)) else [b_]) for a_, b_ in srcs]
            if part in ("all", "dve"):
              for j, (src, sb_) in enumerate(srcs):
                P.op(DVE, lambda e, j=j, src=src: e.bn_stats(out=stt[:, j, 0, :], in_=src[:, 0:512]),
                     reads=sb_, writes=[b_st[j]])
                P.op(DVE, lambda e, j=j, src=src: e.bn_stats(out=stt[:, j, 1, :], in_=src[:, 512:1024]),
                     reads=sb_, writes=[b_st[j]])
                P.op(DVE, lambda e, j=j: e.bn_aggr(out=mv[:, j, :], in_=stt[:, j, :, :]),
                     reads=[b_st[j]], writes=[b_mv])
            if part == "dve":
                return
            P.op(ACT, lambda e: e.activation(out=lnt[:, 0:nb, 0], in_=mv[:, 0:nb, 1], func=AF.Ln,
                                             bias=sc[:, 0:1], scale=1.0), reads=[b_mv, b_sc], writes=[b_lnt])
            P.op(ACT, lambda e: e.activation(out=lnt[:, 0:nb, 1], in_=lnt[:, 0:nb, 0], func=AF.Exp, scale=-0.5),
                 reads=[b_lnt], writes=[b_lnt])
            P.op(DVE, lambda e: e.scalar_tensor_tensor(out=lnt[:, 0:nb, 2], in0=mv[:, 0:nb, 0], scalar=-1.0,
                                                       in1=lnt[:, 0:nb, 1], op0=ALU.mult, op1=ALU.mult),
                 reads=[b_mv, b_lnt], writes=[b_lnt])

        def ln_apply_bf(srcs, zt=None, zbufs=None, eng=ACT):
            zt = zb if zt is None else zt
            zbufs = b_zb if zbufs is None else zbufs
            srcs = [(a_, list(b_) if isinstance(b_, (tuple, list)) else [b_]) for a_, b_ in srcs]
            for j, (src, sb_) in enumerate(srcs):
                if eng == DVE:
                    P.op(DVE, lambda e, j=j, src=src, zt=zt: e.tensor_scalar(
                        out=zt[:, j, :], in0=src, scalar1=lnt[:, j, 1:2], scalar2=lnt[:, j, 2:3],
                        op0=ALU.mult, op1=ALU.add), reads=sb_ + [b_lnt], writes=[zbufs[j]])
                    continue
                P.op(ACT, lambda e, j=j, src=src, zt=zt: e.activation(out=zt[:, j, :], in_=src, func=AF.Identity,
                                                                      bias=lnt[:, j, 2:3], scale=lnt[:, j, 1:2]),
                     reads=sb_ + [b_lnt], writes=[zbufs[j]])

        def ln_transposes(nb, gcol, bcol, tbanks, zt=None, zbufs=None, evac="mix"):
            zt = zb if zt is None else zt
            zbufs = b_zb if zbufs is None else zbufs
            for (j0, j1) in ((0, min(nb, 4)), (4, nb)):
                if j1 <= j0:
                    continue
                n = (j1 - j0) * 128
                c0 = j0 * 128
                for dt in range(8):
                    bk = tbanks[dt % 2]
                    tpb = bank_bf(bk)
                    tp = tpb[:, 0:n]
                    for j in range(j0, j1):
                        P.op(PE, lambda e, j=j, dt=dt, tpb=tpb, j0=j0, zt=zt: e.transpose(
                            out=tpb[:, (j - j0) * 128:(j - j0 + 1) * 128],
                            in_=zt[:, j, dt * 128:(dt + 1) * 128], identity=ident),
                            reads=[zbufs[j], b_ident], writes=[REGB[bk]])
                    if dt % 2 == 0 and evac == "mix":
                        P.op(DVE, lambda e, dt=dt, tp=tp, n=n, c0=c0: e.tensor_scalar(
                            out=xT[:, dt, c0:c0 + n], in0=tp, scalar1=colp[:, gcol + dt:gcol + dt + 1],
                            scalar2=colp[:, bcol + dt:bcol + dt + 1], op0=ALU.mult, op1=ALU.add),
                            reads=[REGB[bk], b_colp], writes=[b_xT[dt]])
                    else:
                        P.op(ACT, lambda e, dt=dt, tp=tp, n=n, c0=c0: e.activation(
                            out=xT[:, dt, c0:c0 + n], in_=tp, func=AF.Identity,
                            bias=colp[:, bcol + dt:bcol + dt + 1], scale=colp[:, gcol + dt:gcol + dt + 1]),
                            reads=[REGB[bk], b_colp], writes=[b_xT[dt]])

        def mm(out, lhsT, rhs, start, stop, reads, writes):
            P.op(PE, lambda e: e.matmul(out, lhsT=lhsT, rhs=rhs, start=start, stop=stop, skip_group_check=True),
                 reads=reads, writes=writes)

        KT = [[AR.alloc([67, SEQ], BF16) for c in range(2)] for hl in range(2)]
        Vt = AR.alloc([128, NBLK, 2, 129], BF16)
        QT = [[AR.alloc([67, 512], BF16) for c in range(2)] for hl in range(2)]
        zb2 = AR.alloc([128, 4, D], BF16)
        b_zb2 = [Buf() for _ in range(4)]
        wq = AR.alloc([128, 8, 256], BF16)
        wk = AR.alloc([128, 8, 256], BF16)
        wv = AR.alloc([128, 8, 256], BF16)
        Pt = [AR.alloc([128, 2, 512], BF16) for _ in range(2)]
        fa32 = AR.alloc([128, 128], F32)
        fatt = AR.alloc([128, 128], F32)
        fattb = AR.alloc([128, 128], BF16)
        frc = AR.alloc([128, 8], F32)
        fst = AR.alloc([128, 6], F32)
        fmv = AR.alloc([128, 4], F32)
        b_KT = [[[Buf() for _ in range(16)] for c in range(2)] for hl in range(2)]
        b_KTrow = Buf()
        b_V = [Buf() for _ in range(16)]
        b_Vone = Buf()
        b_QT = [[Buf() for c in range(2)] for hl in range(2)]
        b_QTrow = [[Buf() for c in range(2)] for hl in range(2)]
        b_wq, b_wk, b_wv = Buf(), Buf(), Buf()
        b_Pt = [Buf(), Buf()]
        b_fa, b_fatt, b_fattb, b_frc, b_fst, b_fmv = [Buf() for _ in range(6)]
        Oreg = []
        b_O = []
        for a_ in range(8):
            Oreg.append(REG[4 + a_ // 2][:, (a_ % 2) * 129:(a_ % 2) * 129 + 129])
            b_O.append(REGB[4 + a_ // 2])

        for hl in range(2):
            for c in range(2):
                dma(SP, KT[hl][c][64:67, :].rearrange("p (g k) -> p g k", k=128),
                    bass.AP(kbrows_d.tensor, 0, [[128, 3], [0, NBLK], [1, 128]]), "kr%d%d" % (hl, c),
                    writes=[b_KTrow])
        ckp('krows')
        P.op(DVE, lambda e: e.memset(Vt[:, :, :, 0:1], 1.0), writes=[b_Vone])
        ckp('rows')

        sp_idx = [0]
        pt_idx = [0]

        for hp in range(2):
            h0 = 2 * hp
            dma(POOL, wq, w_in[:, h0 * 128:h0 * 128 + 256].rearrange("(kt p) n -> p kt n", p=128), "wq",
                writes=[b_wq])
            dma(POOL, wk, w_in[:, 512 + h0 * 128:512 + h0 * 128 + 256].rearrange("(kt p) n -> p kt n", p=128),
                "wk", writes=[b_wk])
            dma(POOL, wv, w_in[:, 1024 + h0 * 128:1024 + h0 * 128 + 256].rearrange("(kt p) n -> p kt n", p=128),
                "wv", writes=[b_wv])
            for hl in range(2):
                for c in range(2):
                    dma(SP, QT[hl][c][64:67, :], qbrows_d[:, (h0 + hl) * 512:(h0 + hl + 1) * 512],
                        "qr%d%d" % (hl, c), writes=[b_QTrow[hl][c]])
            if hp == 0:
                ckp('wA')

            def finalize(hl, J, i):
                h = h0 + hl
                tb = 4 * J + i
                o1, o2 = Oreg[2 * i], Oreg[2 * i + 1]
                bo1, bo2 = b_O[2 * i], b_O[2 * i + 1]
                P.op(DVE, lambda e: e.reciprocal(out=frc[:, 0:1], in_=o1[:, 0:1]), reads=[bo1], writes=[b_frc])
                P.op(DVE, lambda e: e.reciprocal(out=frc[:, 1:2], in_=o2[:, 0:1]), reads=[bo2], writes=[b_frc])
                P.op(DVE, lambda e: e.tensor_tensor(out=frc[:, 2:3], in0=frc[:, 1:2], in1=sc[:, 5:6], op=ALU.mult),
                     reads=[b_frc, b_sc], writes=[b_frc])
                P.op(DVE, lambda e: e.tensor_scalar(out=fa32, in0=o1[:, 1:129], scalar1=frc[:, 0:1], scalar2=None,
                                                    op0=ALU.mult), reads=[bo1, b_frc], writes=[b_fa])
                P.op(DVE, lambda e: e.scalar_tensor_tensor(out=fatt, in0=o2[:, 1:129], scalar=frc[:, 2:3], in1=fa32,
                                                           op0=ALU.mult, op1=ALU.add),
                     reads=[bo2, b_frc, b_fa], writes=[b_fatt])
                P.op(DVE, lambda e: e.bn_stats(out=fst, in_=fatt), reads=[b_fatt], writes=[b_fst])
                P.op(DVE, lambda e: e.bn_aggr(out=fmv[:, 0:2], in_=fst), reads=[b_fst], writes=[b_fmv])
                P.op(DVE, lambda e: e.scalar_tensor_tensor(out=fmv[:, 2:3], in0=fmv[:, 0:1], scalar=fmv[:, 0:1],
                                                           in1=fmv[:, 1:2], op0=ALU.mult, op1=ALU.add),
                     reads=[b_fmv], writes=[b_fmv])
                P.op(ACT, lambda e: e.activation(out=fmv[:, 3:4], in_=fmv[:, 2:3], func=AF.Ln, bias=sc[:, 0:1],
                                                 scale=1.0), reads=[b_fmv, b_sc], writes=[b_fmv])
                P.op(ACT, lambda e: e.activation(out=fmv[:, 3:4], in_=fmv[:, 3:4], func=AF.Exp, scale=-0.5),
                     reads=[b_fmv], writes=[b_fmv])
                P.op(DVE, lambda e: e.tensor_scalar(out=fattb, in0=fatt, scalar1=fmv[:, 3:4], scalar2=None,
                                                    op0=ALU.mult), reads=[b_fatt, b_fmv], writes=[b_fattb])
                P.op(PE, lambda e: e.transpose(out=bank_bf(3)[:, 0:128], in_=fattb, identity=ident),
                     reads=[b_fattb, b_ident], writes=[REGB[3]])
                P.op(ACT, lambda e: e.activation(out=attT[:, h, tb * 128:(tb + 1) * 128], in_=bank_bf(3)[:, 0:128],
                                                 func=AF.Identity, scale=sc[:, 6:7]),
                     reads=[REGB[3], b_sc], writes=[b_attT[h]])

            jobs = []
            for J_ in range(NQT):
                jobs += [("kv", J_, 0), ("kv", J_, 1), ("q", J_, 0)]
            zsets = [(zb, b_zb), (zb2, b_zb2)]

            def job_src(job):
                kind, J_, half = job
                if kind == "kv":
                    g0_ = 8 * J_ + 4 * half
                    return xkv[g0_ * 128:(g0_ + 4) * 128, :]
                return xown[J_ * 512:(J_ + 1) * 512, :]

            def job_load(ji):
                load_x(job_src(jobs[ji]), 4)

            def job_stats(ji, part="all"):
                ln_stats([(R[:, j, :], b_R[j]) for j in range(4)], 4, part)

            def job_apply(ji, eng=ACT):
                zt, zbufs = zsets[ji % 2]
                ln_apply_bf([(R[:, j, :], b_R[j]) for j in range(4)], zt, zbufs, eng)

            def job_back(ji):
                kind, J_, half = jobs[ji]
                zt, zbufs = zsets[ji % 2]
                nxt = ji + 1 if ji + 1 < len(jobs) else None
                if nxt is not None:
                    job_load(nxt)
                    job_stats(nxt, "dve")
                ln_transposes(4, C_G0, C_B0, (2, 3), zt, zbufs, evac="act")
                if nxt is not None:
                    job_stats(nxt, "act")
                    job_apply(nxt, DVE)
                wsel, bsel = (wk, b_wk) if kind == "kv" else (wq, b_wq)
                grp_ = 2 * J_ + half
                g0_ = 8 * J_ + 4 * half
                ri = 0
                for hl in range(2):
                    for c in range(2):
                        r = ri % 2
                        ri += 1
                        co = (hl * 2 + c) * 64
                        for dt in range(8):
                            mm(REG[r][0:64, :], wsel[:, dt, co:co + 64], xT[:, dt, 0:512], dt == 0, dt == 7,
                               [bsel, b_xT[dt]], [REGB[r]])
                        if kind == "kv":
                            P.op(ACT, lambda e, hl=hl, c=c, r=r, g0_=g0_: e.activation(
                                out=KT[hl][c][0:64, g0_ * 128:(g0_ + 4) * 128], in_=REG[r][0:64, :], func=AF.Copy),
                                reads=[REGB[r]], writes=[b_KT[hl][c][grp_]])
                        else:
                            P.op(ACT, lambda e, hl=hl, c=c, r=r: e.activation(
                                out=QT[hl][c][0:64, :], in_=REG[r][0:64, :], func=AF.Identity, scale=0.125),
                                reads=[REGB[r]], writes=[b_QT[hl][c]])
                if kind == "kv":
                    for j in range(4):
                        r = j % 2
                        for dt in range(8):
                            mm(REG[r][:, 0:256], xT[:, dt, j * 128:(j + 1) * 128], wv[:, dt, :], dt == 0, dt == 7,
                               [b_wv, b_xT[dt]], [REGB[r]])
                        P.op(DVE, lambda e, r=r, g=g0_ + j: e.tensor_copy(
                            out=Vt[:, g, :, 1:129], in_=REG[r][:, 0:256].rearrange("p (a b) -> p a b", a=2)),
                            reads=[REGB[r], b_Vone], writes=[b_V[grp_]])

            job_load(0)
            job_stats(0)
            job_apply(0)
            for J in range(NQT):
                for jj in range(3):
                    job_back(3 * J + jj)
                for hl in range(2):
                    h = h0 + hl
                    for i in range(4):
                        P.op(DVE, lambda e, i=i: e.memset(REG[4 + i][:, 0:258], 0.0), writes=[REGB[4 + i]])
                    nunits = 8 * J + 8
                    spis = []

                    def unit_geo(g):
                        m = g - 8 * J
                        i0 = 0 if m < 0 else m // 2
                        return m, i0, i0 * 128

                    def emit_qk(g):
                        m, i0, q0 = unit_geo(g)
                        grp = g // 4
                        spi = sp_idx[0] % 2
                        sp_idx[0] += 1
                        spis.append(spi)
                        Sps = PSt[spi][:, :]
                        sb2 = [REGB[2 * spi], REGB[2 * spi + 1]]
                        for c in range(2):
                            mm(Sps[:, c * 512 + q0:(c + 1) * 512], KT[hl][c][0:67, g * 128:(g + 1) * 128],
                               QT[hl][c][0:67, q0:512], True, True,
                               [b_KT[hl][c][grp], b_KTrow, b_QT[hl][c], b_QTrow[hl][c]], [sb2[c]])

                    def emit_exp(g):
                        m, i0, q0 = unit_geo(g)
                        spi = spis[g]
                        Sps = PSt[spi][:, :]
                        sb2 = [REGB[2 * spi], REGB[2 * spi + 1]]
                        if m >= 0:
                            mo = (m % 2) * 128
                            for c in range(2):
                                P.op(DVE, lambda e, c=c, Sps=Sps, q0=q0, mo=mo: e.tensor_tensor(
                                    out=Sps[:, c * 512 + q0:c * 512 + q0 + 128],
                                    in0=Sps[:, c * 512 + q0:c * 512 + q0 + 128],
                                    in1=masks[:, mo:mo + 128], op=ALU.add),
                                    reads=[sb2[c], b_masks], writes=[sb2[c]])
                        pti = pt_idx[0] % 2
                        pt_idx[0] += 1
                        bcol = h * 64 + (m + BT_OFF)
                        P.op(ACT, lambda e, Sps=Sps, pti=pti, q0=q0, bcol=bcol: e.activation(
                            out=Pt[pti][:, :, q0:512],
                            in_=Sps.rearrange("p (a b) -> p a b", a=2)[:, :, q0:512],
                            func=AF.Exp, bias=btab[:, bcol:bcol + 1], scale=1.0),
                            reads=[sb2[0], sb2[1], b_btab], writes=[b_Pt[pti]])
                        return pti

                    def emit_pv(g, pti):
                        m, i0, q0 = unit_geo(g)
                        grp = g // 4
                        for i in range(i0, 4):
                            last = 8 * J + 2 * i + 1
                            for c in range(2):
                                a = 2 * i + c
                                mm(Oreg[a], Pt[pti][:, c, i * 128:(i + 1) * 128], Vt[:, g, hl, :],
                                   False, g == last, [b_Pt[pti], b_V[grp], b_Vone], [b_O[a]])
                        if m >= 0 and m % 2 == 1:
                            finalize(hl, J, m // 2)

                    emit_qk(0)
                    for g in range(nunits):
                        pti = emit_exp(g)
                        if g + 1 < nunits:
                            emit_qk(g + 1)
                        emit_pv(g, pti)
                    if J == 0 and hl == 0 and hp == 0:
                        ckp('att00')
                if J == 0 and hp == 0:
                    ckp('attJ0')
            if hp == 0:
                ckp('hp0')

        ckp('A')
        P.barrier()
        AR.reset(pmark)
        NSLOT = 4
        Wr = [AR.alloc([128, 8, 512], BF16) for _ in range(NSLOT)]
        b_W = [Buf() for _ in range(NSLOT)]
        hT = AR.alloc([128, NFT, 512], BF16)
        b_hT = [Buf() for _ in range(NFT)]
        qcT = hT[:, 0:8, :]
        caT = hT[:, 8:16, :]
        gbt = [AR.alloc([128, D], F32) for _ in range(2)]
        b_gbt = [Buf(), Buf()]
        R2 = AR.alloc([128, 4, D], F32)
        b_R2 = [Buf() for _ in range(4)]
        gluT = AR.alloc([128, 4, 4, 160], BF16)
        b_glu = [Buf() for _ in range(4)]
        tg = hT[:, 11:14, :].rearrange("p a b -> p (a b)").bitcast(F32)[:, 0:640]
        b_tg = Buf()
        _ct = hT[:, 14:22, :].rearrange("p a b -> p (a b)").bitcast(F32).rearrange("p (a b) -> p a b", a=4)
        y32 = _ct[:, 0, :]
        yb = AR.alloc([128, 512], BF16)
        ysq = AR.alloc([128, 512], BF16)
        m2 = _ct[:, 1, :]
        var = _ct[:, 2, :]
        dd = _ct[:, 3, :]
        b_y32, b_yb, b_ysq, b_m2, b_var, b_dd = [Buf() for _ in range(6)]
        siluT = AR.alloc([128, 4, 512], BF16)
        b_silu = [Buf() for _ in range(4)]
        cT = AR.alloc([128, 4, 512], BF16)
        b_cT = [Buf() for _ in range(4)]
        NDG = 8
        Dg = AR.alloc([128, NDG, 128], BF16)
        b_Dg = [Buf() for _ in range(NDG)]
        PTc = [AR.alloc([128, 2, 512], BF16) for _ in range(2)]
        b_PTc = [Buf(), Buf()]
        rinv = AR.alloc([128, 512], F32)
        b_rinv = Buf()
        sgx = AR.alloc([128, D], F32)
        sg = [sgx[:, 0:512], sgx[:, 512:1024]]
        b_sg = [Buf(), Buf()]
        xh32 = AR.alloc([128, D], F32)
        b_xh = Buf()
        kcT = AR.alloc([128, 8, MEM], BF16)
        vc = AR.alloc([128, 2, D], BF16)
        b_kcT, b_vc = Buf(), Buf()

        wslot = [0]
        gslot = [0]

        def load_slab(src, kt):
            s = wslot[0] % NSLOT
            wslot[0] += 1
            ncol = src.shape[1]
            dma(POOL, Wr[s][:, 0:kt, 0:ncol], src.rearrange("(kt p) n -> p kt n", p=128), "w%d" % s,
                writes=[b_W[s]])
            return s

        def load_row_bc(row):
            s = gslot[0] % 2
            gslot[0] += 1
            dma(SP, gbt[s], bass.AP(rowp_d.tensor, row * D, [[0, 128], [1, D]]), "g%d" % s, writes=[b_gbt[s]])
            return s

        memT = hT[:, 0:4, :].rearrange("p a b -> p (a b)").rearrange("p (a b) -> p a b", a=8)
        b_memT = Buf()
        dma(SP, R[:, 0:2, :], memd.rearrange("(j p) d -> p j d", p=128), "xR", writes=[b_R[0], b_R[1]])
        for j in range(2):
            P.op(ACT, lambda e, j=j, R=R: e.activation(out=zb[:, j, :], in_=R[:, j, :], func=AF.Copy),
                 reads=[b_R[j]], writes=[b_zb[j]])
        for dt in range(8):
            bk = 6 + dt % 2
            for j in range(2):
                P.op(PE, lambda e, j=j, dt=dt, bk=bk: e.transpose(
                    out=bank_bf(bk)[:, j * 128:(j + 1) * 128],
                    in_=zb[:, j, dt * 128:(dt + 1) * 128], identity=ident),
                    reads=[b_zb[j], b_ident], writes=[REGB[bk]])
            P.op(DVE, lambda e, dt=dt, bk=bk: e.tensor_copy(out=memT[:, dt, :], in_=bank_bf(bk)[:, 0:256]),
                 reads=[REGB[bk]], writes=[b_memT])
        for q4 in range(2):
            s = load_slab(w_kv_mem[:, q4 * 512:(q4 + 1) * 512], 8)
            for nn in range(4):
                nt = q4 * 4 + nn
                r = nt % 3
                for dt in range(8):
                    mm(REG[r][:, 0:MEM], Wr[s][:, dt, nn * 128:(nn + 1) * 128], memT[:, dt, :], dt == 0, dt == 7,
                       [b_W[s], b_memT], [REGB[r]])
                P.op(ACT, lambda e, r=r, nt=nt: e.activation(out=kcT[:, nt, :], in_=REG[r][:, 0:MEM], func=AF.Copy),
                     reads=[REGB[r]], writes=[b_kcT])
        for q4 in range(2):
            s = load_slab(w_kv_mem[:, D + q4 * 512:D + (q4 + 1) * 512], 8)
            for mt in range(2):
                r = 3 + mt
                for dt in range(8):
                    mm(REG[r], memT[:, dt, mt * 128:(mt + 1) * 128], Wr[s][:, dt, :], dt == 0, dt == 7,
                       [b_W[s], b_memT], [REGB[r]])
                P.op(DVE, lambda e, r=r, mt=mt, q4=q4: e.tensor_copy(out=vc[:, mt, q4 * 512:(q4 + 1) * 512],
                                                                      in_=REG[r]),
                     reads=[REGB[r]], writes=[b_vc])

        ckp('mem')
        P.barrier()

        def residual_mm(li, lhs_tiles, lhs_bufs, wsrc, nk):
            nslab = (nk + 7) // 8
            for nh in range(2):
                slabs = []
                for s_ in range(nslab):
                    k0 = s_ * 8
                    kt = min(8, nk - k0)
                    slabs.append((load_slab(wsrc[k0 * 128:(k0 + kt) * 128, nh * 512:(nh + 1) * 512], kt), k0, kt))
                for si, (s, k0, kt) in enumerate(slabs):
                    for tb in range(4):
                        r = tb
                        for kk in range(kt):
                            f = k0 + kk
                            mm(REG[r], lhs_tiles(f)[:, tb * 128:(tb + 1) * 128], Wr[s][:, kk, :],
                               f == 0, f == nk - 1, [lhs_bufs[f], b_W[s]], [REGB[r]])
                        if si == nslab - 1:
                            P.op(DVE, lambda e, tb=tb, r=r, nh=nh, R=R: e.scalar_tensor_tensor(
                                out=R[:, tb, nh * 512:(nh + 1) * 512], in0=R[:, tb, nh * 512:(nh + 1) * 512],
                                scalar=ALPHA, in1=REG[r], op0=ALU.mult, op1=ALU.add),
                                reads=[b_R[tb], REGB[r]], writes=[b_R[tb]])

        def x_z(nb=4):
            for j in range(nb):
                P.op(DVE, lambda e, j=j, R=R: e.tensor_scalar(out=R[:, j, :], in0=R[:, j, :], scalar1=lnt[:, j, 1:2],
                                                              scalar2=lnt[:, j, 2:3], op0=ALU.mult, op1=ALU.add),
                     reads=[b_lnt], writes=[b_R[j]])

        def x_affine(grow, brow):
            sg_ = load_row_bc(grow)
            sb_ = load_row_bc(brow)
            for j in range(4):
                P.op(DVE, lambda e, j=j, sg_=sg_, R=R: e.tensor_tensor(out=R[:, j, :], in0=R[:, j, :], in1=gbt[sg_],
                                                                       op=ALU.mult),
                     reads=[b_gbt[sg_]], writes=[b_R[j]])
                P.op(DVE, lambda e, j=j, sb_=sb_, R=R: e.tensor_tensor(out=R[:, j, :], in0=R[:, j, :], in1=gbt[sb_],
                                                                       op=ALU.add),
                     reads=[b_gbt[sb_]], writes=[b_R[j]])

        def ln_full(gcol, bcol, grow, brow):
            srcs = [(R[:, j, :], b_R[j]) for j in range(4)]
            ln_stats(srcs, 4)
            ln_apply_bf(srcs)
            ln_transposes(4, gcol, bcol, (6, 7))
            P.capture_begin()
            x_z()
            x_affine(grow, brow)
            return P.capture_end()

        def prologue_load(ck_):
            load_x(xown[ck_ * 512:(ck_ + 1) * 512, :], 4)
            dma(SP, xh32, xhalo[ck_ * 128:(ck_ + 1) * 128, :], "xh", writes=[b_xh])

        def prologue_a(ck_):
            srcs = [(R[:, j, :], b_R[j]) for j in range(4)] + [(xh32, b_xh)]
            ln_stats(srcs, 5)
            ln_apply_bf(srcs)

        def prologue_b(ck_):
            ln_transposes(5, C_G0, C_B0, (6, 7))
            x_z()
            x_affine(0, 1)

        def epilogue(ck_):
            srcs = [(R[:, j, :], b_R[j]) for j in range(4)]
            ln_stats(srcs, 4)
            x_z()
            x_affine(6, 7)
            final_ops.append(dma(SP, y[ck_ * 512:(ck_ + 1) * 512, :].rearrange("(j p) d -> p j d", p=128), R, "out",
                                 reads=b_R))

        dg_idx = [0]

        Rs = [(R, b_R), (R2, b_R2)]
        prologue_load(0)
        prologue_a(0)
        prologue_b(0)
        for ck in range(NQT):
            R, b_R = Rs[ck % 2]
            ckp('ln0')
            s_val = load_slab(w_in[:, 1536:2048], 8)
            s_gate = load_slab(w_in[:, 2048:2560], 8)
            s_pw = load_slab(w_pw, 4)
            for ct in range(4):
                pv, pg = PSt[0], PSt[1]
                for (ps_, s_, rb) in ((pv, s_val, (REGB[0], REGB[1])), (pg, s_gate, (REGB[2], REGB[3]))):
                    for dt in range(8):
                        mm(ps_[:, 0:512], Wr[s_][:, dt, ct * 128:(ct + 1) * 128], xT[:, dt, 0:512], dt == 0, dt == 7,
                           [b_W[s_], b_xT[dt]], [rb[0]])
                    for dt in range(8):
                        mm(ps_[:, 512:640], Wr[s_][:, dt, ct * 128:(ct + 1) * 128], xT[:, dt, 512:640], dt == 0,
                           dt == 7, [b_W[s_], b_xT[dt]], [rb[1]])
                P.op(ACT, lambda e: e.activation(out=tg, in_=PSt[1][:, 0:640], func=AF.Tanh, scale=0.5),
                     reads=[REGB[2], REGB[3]], writes=[b_tg])
                P.op(DVE, lambda e, ct=ct: e.scalar_tensor_tensor(
                    out=gluT[:, ct, :, 32:160], in0=tg[:, 0:512].rearrange("p (a b) -> p a b", a=4), scalar=1.0,
                    in1=PSt[0][:, 0:512].rearrange("p (a b) -> p a b", a=4), op0=ALU.add, op1=ALU.mult),
                    reads=[b_tg, REGB[0]], writes=[b_glu[ct]])
                P.op(DVE, lambda e, ct=ct: e.scalar_tensor_tensor(
                    out=gluT[:, ct, :, 0:32], in0=tg[:, 512:640].rearrange("p (a b) -> p a b", a=4), scalar=1.0,
                    in1=PSt[0][:, 512:640].rearrange("p (a b) -> p a b", a=4), op0=ALU.add, op1=ALU.mult),
                    reads=[b_tg, REGB[1]], writes=[b_glu[ct]])
                if ck == 0:
                    P.op(DVE, lambda e, ct=ct: e.tensor_scalar(out=gluT[:, ct, 0, 0:32], in0=gluT[:, ct, 0, 0:32],
                                                               scalar1=hmask[:, 0:1], scalar2=None, op0=ALU.mult),
                         reads=[b_hmask], writes=[b_glu[ct]])
            ckp('glu')
            for ct in range(4):
                for k in range(31):
                    di = dg_idx[0] % NDG
                    dg_idx[0] += 1
                    wc = C_CW + ct * 31 + k
                    P.op(DVE, lambda e, di=di, wc=wc: e.tensor_scalar(out=Dg[:, di, :], in0=ident,
                                                                       scalar1=colp[:, wc:wc + 1], scalar2=None,
                                                                       op0=ALU.mult),
                         reads=[b_ident, b_colp], writes=[b_Dg[di]])
                    mm(REG[4].rearrange("p (a b) -> p a b", a=4), Dg[:, di, :], gluT[:, ct, :, 2 + k:2 + k + 128],
                       k == 0, k == 30, [b_Dg[di], b_glu[ct]], [REGB[4]])
                P.op(ACT, lambda e, ct=ct: e.activation(out=y32, in_=REG[4], func=AF.Identity,
                                                        bias=colp[:, C_CB + ct:C_CB + ct + 1], scale=0.5),
                     reads=[REGB[4], b_colp], writes=[b_y32])
                P.op(DVE, lambda e: e.tensor_copy(out=yb, in_=y32), reads=[b_y32], writes=[b_yb])
                P.op(ACT, lambda e: e.activation(out=ysq, in_=y32, func=AF.Square), reads=[b_y32], writes=[b_ysq])
                mm(REG[5], onesm_bf, yb, True, True, [b_ones, b_yb], [REGB[5]])
                mm(REG[6], onesm_bf, ysq, True, True, [b_ones, b_ysq], [REGB[6]])
                P.op(ACT, lambda e: e.activation(out=m2, in_=REG[5], func=AF.Square), reads=[REGB[5]], writes=[b_m2])
                P.op(DVE, lambda e: e.tensor_tensor(out=var, in0=REG[6], in1=m2, op=ALU.subtract),
                     reads=[REGB[6], b_m2], writes=[b_var])
                P.op(ACT, lambda e: e.activation(out=var, in_=var, func=AF.Ln, bias=sc[:, 0:1], scale=1.0),
                     reads=[b_sc], writes=[b_var])
                P.op(ACT, lambda e: e.activation(out=var, in_=var, func=AF.Exp, scale=-0.5), writes=[b_var])
                P.op(DVE, lambda e: e.tensor_tensor(out=dd, in0=y32, in1=REG[5], op=ALU.subtract),
                     reads=[b_y32, REGB[5]], writes=[b_dd])
                P.op(DVE, lambda e: e.tensor_tensor(out=dd, in0=dd, in1=var, op=ALU.mult),
                     reads=[b_var], writes=[b_dd])
                P.op(ACT, lambda e, ct=ct: e.activation(out=siluT[:, ct, :], in_=dd, func=AF.Silu,
                                                        bias=colp[:, C_CNB + ct:C_CNB + ct + 1],
                                                        scale=colp[:, C_CNG + ct:C_CNG + ct + 1]),
                     reads=[b_dd, b_colp], writes=[b_silu[ct]])
            for nt in range(4):
                r = nt % 4
                for ct in range(4):
                    mm(REG[r], Wr[s_pw][:, ct, nt * 128:(nt + 1) * 128], siluT[:, ct, :], ct == 0, ct == 3,
                       [b_W[s_pw], b_silu[ct]], [REGB[r]])
                P.op(ACT, lambda e, nt=nt, r=r: e.activation(out=cT[:, nt, :], in_=REG[r], func=AF.Identity,
                                                             bias=colp[:, C_BPW + nt:C_BPW + nt + 1], scale=1.0),
                     reads=[REGB[r], b_colp], writes=[b_cT[nt]])
            ckp('conv')
            residual_mm(0, lambda f: (attT[:, f, ck * 512:(ck + 1) * 512] if f < 4 else cT[:, f - 4, :]),
                        [b_attT[0], b_attT[1], b_attT[2], b_attT[3]] + b_cT, w_o, 8)
            cap_x = ln_full(C_G1, C_B1, 2, 3)
            ckp('mix')
            for q4 in range(2):
                s = load_slab(w_q_mem[:, q4 * 512:(q4 + 1) * 512], 8)
                for nn in range(4):
                    nt = q4 * 4 + nn
                    r = 4 + nt % 3
                    for dt in range(8):
                        mm(REG[r], Wr[s][:, dt, nn * 128:(nn + 1) * 128], xT[:, dt, 0:512], dt == 0, dt == 7,
                           [b_W[s], b_xT[dt]], [REGB[r]])
                    P.op(ACT, lambda e, nt=nt, r=r: e.activation(out=qcT[:, nt, :], in_=REG[r], func=AF.Identity,
                                                                 scale=1.0 / 16.0),
                         reads=[REGB[r]], writes=[b_hT[nt]])
                    P.replay(cap_x, 1)
            for hh in range(4):
                pi = hh % 2
                for mt in range(2):
                    r = mt
                    for e2 in range(2):
                        nt = 2 * hh + e2
                        mm(REG[r], kcT[:, nt, mt * 128:(mt + 1) * 128], qcT[:, nt, :], e2 == 0, e2 == 1,
                           [b_kcT, b_hT[nt]], [REGB[r]])
                    P.op(ACT, lambda e, pi=pi, mt=mt, r=r: e.activation(out=PTc[pi][:, mt, :], in_=REG[r],
                                                                         func=AF.Exp),
                         reads=[REGB[r]], writes=[b_PTc[pi]])
                for mt in range(2):
                    mm(REG[4], ones_bf, PTc[pi][:, mt, :], mt == 0, mt == 1, [b_ones, b_PTc[pi]], [REGB[4]])
                P.op(DVE, lambda e: e.reciprocal(out=rinv, in_=REG[4]), reads=[REGB[4]], writes=[b_rinv])
                P.replay(cap_x, 1)
                for e2 in range(2):
                    nt = 2 * hh + e2
                    r = 2 + e2
                    for mt in range(2):
                        mm(REG[r], vc[:, mt, nt * 128:(nt + 1) * 128], PTc[pi][:, mt, :], mt == 0, mt == 1,
                           [b_vc, b_PTc[pi]], [REGB[r]])
                    P.op(DVE, lambda e, nt=nt, r=r: e.tensor_tensor(out=caT[:, nt, :], in0=REG[r], in1=rinv,
                                                                    op=ALU.mult),
                         reads=[REGB[r], b_rinv], writes=[b_hT[8 + nt]])
            P.replay(cap_x, 1000)
            residual_mm(1, lambda f: caT[:, f, :], [b_hT[8 + f] for f in range(8)], w_o_mem, 8)
            cap_x = ln_full(C_G2, C_B2, 4, 5)
            ckp('ca')
            cap_e, cap_p = [], []
            if ck > 0:
                R, b_R = Rs[(ck - 1) % 2]
                P.capture_begin()
                epilogue(ck - 1)
                cap_e = P.capture_end()
                R, b_R = Rs[ck % 2]
            ft = 0
            for fg in range(6):
                ncol = 512 if fg < 5 else 256
                sg_s = load_slab(w_gate[:, fg * 512:fg * 512 + ncol], 8)
                su_s = load_slab(w_up[:, fg * 512:fg * 512 + ncol], 8)
                for nn in range(ncol // 128):
                    rg = 2 * (ft % 3)
                    ru = rg + 1
                    for dt in range(8):
                        mm(REG[rg], Wr[sg_s][:, dt, nn * 128:(nn + 1) * 128], xT[:, dt, 0:512], dt == 0, dt == 7,
                           [b_W[sg_s], b_xT[dt]], [REGB[rg]])
                    for dt in range(8):
                        mm(REG[ru], Wr[su_s][:, dt, nn * 128:(nn + 1) * 128], xT[:, dt, 0:512], dt == 0, dt == 7,
                           [b_W[su_s], b_xT[dt]], [REGB[ru]])
                    si = ft % 2
                    P.op(ACT, lambda e, si=si, rg=rg: e.activation(out=sg[si], in_=REG[rg], func=AF.Silu),
                         reads=[REGB[rg]], writes=[b_sg[si]])
                    P.op(DVE, lambda e, si=si, ru=ru, ft=ft: e.tensor_tensor(out=hT[:, ft, :], in0=sg[si],
                                                                               in1=REG[ru], op=ALU.mult),
                         reads=[b_sg[si], REGB[ru]], writes=[b_hT[ft]])
                    ft += 1
                    if cap_x:
                        P.replay(cap_x, 2)
                    else:
                        P.replay(cap_e, 3)
                    if ft == 11 and ck + 1 < NQT:
                        P.replay(cap_x, 1000)
                        P.replay(cap_e, 1000)
                        R, b_R = Rs[(ck + 1) % 2]
                        prologue_load(ck + 1)
                        P.capture_begin()
                        prologue_a(ck + 1)
                        cap_p = P.capture_end()
                        R, b_R = Rs[ck % 2]
                    if ft > 11:
                        P.replay(cap_p, 3)
            P.replay(cap_x, 1000)
            P.replay(cap_e, 1000)
            P.replay(cap_p, 1000)
            if ck + 1 < NQT:
                R, b_R = Rs[(ck + 1) % 2]
                prologue_b(ck + 1)
                R, b_R = Rs[ck % 2]
            residual_mm(2, lambda f: hT[:, f, :], b_hT, w_down, NFT)
            ckp('ffn')
            ckp('out0')

        R, b_R = Rs[(NQT - 1) % 2]
        epilogue(NQT - 1)
        P.emit(final_ops=final_ops[-1:])
    return nc


_CACHE = {}


def _consts(parity):
    bf = ml_dtypes.bfloat16
    ident = np.eye(128, dtype=np.float32).astype(bf)
    kp = np.arange(128)[:, None]
    qp = np.arange(128)[None, :]
    diag = np.where(kp <= qp, 0.0, NEG).astype(np.float32)
    full = np.full((128, 128), NEG, np.float32)
    zero = np.zeros((128, 128), np.float32)
    if parity == 0:
        masks = np.concatenate([diag, full], axis=1)
    else:
        masks = np.concatenate([zero, diag], axis=1)
    kbrows = np.stack([np.arange(128, dtype=np.float32), np.ones(128, np.float32), np.ones(128, np.float32)]).astype(bf)
    qb = np.zeros((3, 4, 512), np.float32)
    col = np.arange(512)
    for h in range(4):
        qb[0, h, :] = SLOPES[h]
        qb[1, h, :] = -SLOPES[h] * (col % 128)
        qb[2, h, :] = -SLOPES[h] * 256.0 * (col // 128)
    qbrows = qb.reshape(3, 2048).astype(bf)
    bt = np.zeros((128, 256), np.float32)
    for h in range(4):
        for dd_ in range(-BT_OFF, 8):
            bt[:, h * 64 + dd_ + BT_OFF] = SLOPES[h] * 128.0 * dd_
    hmask = np.full((128, 1), 0.0 if parity == 0 else 1.0, np.float32)
    return dict(ident=ident, masks=masks, kbrows=kbrows, qbrows=qbrows, btab=bt, hmask=hmask)


def kernel(x, mem, in_norm_g, in_norm_b, w_in, lambda_q1, lambda_k1, lambda_q2, lambda_k2,
           subln_g, conv_w, conv_b, conv_norm_g, conv_norm_b, w_pw, b_pw, w_o, ln1_g, ln1_b,
           w_q_mem, w_kv_mem, w_o_mem, ln2_g, ln2_b, w_gate, w_up, w_down, ln3_g, ln3_b):
    f32 = np.float32
    x = np.asarray(x, f32)
    mem = np.asarray(mem, f32)

    def colv(v, n):
        return np.asarray(v, f32).reshape(n, 128).T

    colp = np.zeros((128, NCOLP), f32)
    colp[:, C_G0:C_G0 + 8] = colv(in_norm_g, 8)
    colp[:, C_B0:C_B0 + 8] = colv(in_norm_b, 8)
    colp[:, C_G1:C_G1 + 8] = colv(ln1_g[0], 8)
    colp[:, C_B1:C_B1 + 8] = colv(ln1_b[0], 8)
    colp[:, C_G2:C_G2 + 8] = colv(ln2_g[0], 8)
    colp[:, C_B2:C_B2 + 8] = colv(ln2_b[0], 8)
    colp[:, C_CB:C_CB + 4] = colv(conv_b[0], 4)
    colp[:, C_CNG:C_CNG + 4] = colv(conv_norm_g[0], 4)
    colp[:, C_CNB:C_CNB + 4] = colv(conv_norm_b[0], 4)
    colp[:, C_BPW:C_BPW + 4] = colv(b_pw[0], 4)
    colp[:, C_SUB] = np.asarray(subln_g[0], f32)
    cw = np.asarray(conv_w[0], f32)
    colp[:, C_CW:C_CW + 124] = cw.reshape(31, 4, 128).transpose(2, 1, 0).reshape(128, 124)
    rowp = np.stack([np.asarray(v, f32).reshape(D) for v in
                     (in_norm_g, in_norm_b, ln1_g[0], ln1_b[0], ln2_g[0], ln2_b[0], ln3_g[0], ln3_b[0])])
    lamv = np.stack([np.asarray(v, f32).reshape(64) for v in (lambda_q1, lambda_k1, lambda_q2, lambda_k2)])

    shared = dict(
        w_in=np.ascontiguousarray(np.asarray(w_in, f32)[0]), w_pw=np.ascontiguousarray(np.asarray(w_pw, f32)[0]),
        w_o=np.ascontiguousarray(np.asarray(w_o, f32)[0]), w_q_mem=np.ascontiguousarray(np.asarray(w_q_mem, f32)[0]),
        w_kv_mem=np.ascontiguousarray(np.asarray(w_kv_mem, f32)[0]),
        w_o_mem=np.ascontiguousarray(np.asarray(w_o_mem, f32)[0]),
        w_gate=np.ascontiguousarray(np.asarray(w_gate, f32)[0]), w_up=np.ascontiguousarray(np.asarray(w_up, f32)[0]),
        w_down=np.ascontiguousarray(np.asarray(w_down, f32)[0]),
        colp=colp, rowp=np.ascontiguousarray(rowp), lamv=np.ascontiguousarray(lamv))

    in_maps = []
    for core in range(8):
        b, par = core // 2, core % 2
        xb = x[b].reshape(NBLK, 128, D)
        own = np.ascontiguousarray(xb[par::2].reshape(NOWN, D))
        halo = np.zeros((32, 32, D), f32)
        for j in range(32):
            g = 2 * j + par
            if g > 0:
                halo[j] = xb[g - 1, 96:128, :]
        m = dict(shared)
        m.update(xkv=np.ascontiguousarray(x[b]), xown=own, xhalo=halo.reshape(1024, D),
                 mem=np.ascontiguousarray(mem[b]))
        m.update(_consts(par))
        in_maps.append(m)

    if "nc" not in _CACHE:
        _CACHE["nc"] = build_program()
    nc = _CACHE["nc"]
    res = run_bass_kernel_spmd(nc, in_maps, core_ids=list(range(8)))
    out = np.zeros((4, SEQ, D), f32)
    ov = out.reshape(4, NBLK, 128, D)
    for core in range(8):
        b, par = core // 2, core % 2
        ov[b, par::2] = np.asarray(res.results[core]["y"], f32).reshape(32, 128, D)
    return out
```

```python
import math
from contextlib import ExitStack

import numpy as np
import ml_dtypes

import concourse.bass as bass
import concourse.mybir as mybir
from concourse.bass_utils import run_bass_kernel_spmd

F32 = mybir.dt.float32
BF16 = mybir.dt.bfloat16
AF = mybir.ActivationFunctionType
ALU = mybir.AluOpType

PE, ACT, DVE, POOL, SP = "pe", "act", "dve", "pool", "sp"
ENGS = (PE, ACT, DVE, POOL, SP)
EPOCH = 12000

D = 1024
SEQ = 8192
NOWN = 4096
NBLK = 64
NQT = 8
DFF = 2816
NFT = 22
MEM = 256
EPS = 1e-5
ALPHA = 2.0 ** 0.25
LAM_INIT = 0.2
SLOPES = [2.0 ** (-2.0 * (h + 1)) for h in range(4)]
NEG = -30000.0
BT_OFF = 56

C_G0, C_B0, C_G1, C_B1, C_G2, C_B2 = 0, 8, 16, 24, 32, 40
C_CB, C_CNG, C_CNB, C_BPW, C_SUB, C_CW = 48, 52, 56, 60, 64, 65
NCOLP = 192


class Buf:
    __slots__ = ("name", "w", "rs", "excl")

    def __init__(self, name="", excl=False):
        self.name = name
        self.w = None
        self.rs = []
        self.excl = excl


class Op:
    __slots__ = ("eng", "fn", "deps", "need", "dma_key", "semslot", "semval")

    def __init__(self, eng, fn, dma_key):
        self.eng = eng
        self.fn = fn
        self.deps = []
        self.need = False
        self.dma_key = dma_key
        self.semslot = None
        self.semval = None


class Prog:
    def __init__(self, nc):
        self.nc = nc
        self.q = {e: [] for e in ENGS}
        self.dmas = []
        self.stopped = False
        self._cap = None

    def capture_begin(self):
        self._cap = []

    def capture_end(self):
        lst, self._cap = self._cap, None
        return lst

    def replay(self, lst, n):
        while n > 0 and lst:
            a = lst.pop(0)
            self.op(*a)
            n -= 1

    def op(self, eng, fn, reads=(), writes=(), dma_key=None, after=()):
        o = Op(eng, fn, dma_key)
        if self.stopped:
            return o
        if self._cap is not None:
            self._cap.append((eng, fn, tuple(reads), tuple(writes), dma_key, tuple(after)))
            return o
        writes = list(writes) + [b for b in reads if b.excl]
        reads = [b for b in reads if not b.excl]
        deps = list(after)
        for b in reads:
            if b.w is not None:
                deps.append(b.w)
        for b in writes:
            if b.w is not None:
                deps.append(b.w)
            deps.extend(b.rs)
        for b in reads:
            if dma_key is None:
                b.rs = [r for r in b.rs if r.eng != eng or r.dma_key is not None]
            b.rs.append(o)
        for b in writes:
            b.w = o
            b.rs = []
        seen = set()
        for d in deps:
            if d is o or id(d) in seen:
                continue
            seen.add(id(d))
            if d.eng == PE and eng == PE:
                continue
            o.deps.append(d)
            d.need = True
        self.q[eng].append(o)
        if dma_key is not None:
            self.dmas.append(o)
        return o

    def barrier(self):
        lasts = [self.q[e][-1] for e in ENGS if self.q[e]]
        lasts += self.dmas
        self.dmas = []
        for e in ENGS:
            self.op(e, lambda h: h.nop(), after=[l for l in lasts])

    def emit(self, final_ops=()):
        nc = self.nc
        cnt = {}
        for e in ENGS:
            n = 0
            for o in self.q[e]:
                if o.dma_key is not None:
                    key = ("dma", o.dma_key)
                    cnt[key] = cnt.get(key, 0) + 16
                    o.semslot = key
                    o.semval = cnt[key]
                elif o.need:
                    o.semslot = (e, n // EPOCH)
                    o.semval = n % EPOCH + 1
                    n += 1
        keys = []
        ks = set()
        for e in ENGS:
            for o in self.q[e]:
                if o.semslot is not None and o.semslot not in ks:
                    ks.add(o.semslot)
                    keys.append(o.semslot)
        self.nsems = len(keys)
        with ExitStack() as st:
            sems = {}
            for i, k in enumerate(keys):
                sems[k] = st.enter_context(nc.semaphore("s%d" % i))
            block = st.enter_context(nc.Block())

            def run(ename, handle):
                waited = {}
                for o in self.q[ename]:
                    for d in o.deps:
                        k = d.semslot
                        if waited.get(k, 0) >= d.semval:
                            continue
                        handle.wait_ge(sems[k], d.semval)
                        waited[k] = d.semval
                    ins = o.fn(handle)
                    if o.semslot is not None:
                        ins.then_inc(sems[o.semslot], 16 if o.dma_key is not None else 1)
                if ename == SP:
                    lastdma = {}
                    for e2 in ENGS:
                        for o2 in self.q[e2]:
                            if o2.dma_key is not None:
                                lastdma[o2.semslot] = o2
                    for o in list(final_ops) + list(lastdma.values()):
                        if o.semslot is None:
                            continue
                        if waited.get(o.semslot, 0) < o.semval:
                            handle.wait_ge(sems[o.semslot], o.semval)
                            waited[o.semslot] = o.semval

            @block.tensor
            def _(h):
                run(PE, h)

            @block.scalar
            def _(h):
                run(ACT, h)

            @block.vector
            def _(h):
                run(DVE, h)

            @block.gpsimd
            def _(h):
                run(POOL, h)

            @block.sync
            def _(h):
                run(SP, h)


class Arena:
    def __init__(self, U, nbytes):
        self.U = U
        self.nbytes = nbytes
        self.off = 0

    def mark(self):
        return self.off

    def reset(self, m):
        self.off = m

    def alloc(self, shape, dt):
        n = 1
        for s in shape[1:]:
            n *= s
        esz = 2 if dt == BF16 else 4
        size = (n * esz + 63) // 64 * 64
        assert self.off + size <= self.nbytes, ("SBUF arena overflow", self.off, size, self.nbytes)
        v = self.U[0:shape[0], self.off // 2:self.off // 2 + (n * esz) // 2]
        self.off += size
        if dt == F32:
            v = v.bitcast(F32)
        if len(shape) == 3:
            v = v.rearrange("p (a b) -> p a b", a=shape[1])
        elif len(shape) == 4:
            v = v.rearrange("p (a b c) -> p a b c", a=shape[1], b=shape[2])
        return v


class _Stop(Exception):
    pass


def build_program(stop=None):
    nc = bass.Bass("TRN2", target_bir_lowering=False)
    P = Prog(nc)

    def ckp(name):
        if stop == name:
            P.stopped = True

    def din(name, shape, dt=F32):
        return nc.dram_tensor(name, list(shape), dt, kind="ExternalInput").ap()

    xkv = din("xkv", [SEQ, D])
    xown = din("xown", [NOWN, D])
    xhalo = din("xhalo", [1024, D])
    memd = din("mem", [MEM, D])
    w_in = din("w_in", [D, 2560])
    w_pw = din("w_pw", [512, 512])
    w_o = din("w_o", [D, D])
    w_q_mem = din("w_q_mem", [D, D])
    w_kv_mem = din("w_kv_mem", [D, 2 * D])
    w_o_mem = din("w_o_mem", [D, D])
    w_gate = din("w_gate", [D, DFF])
    w_up = din("w_up", [D, DFF])
    w_down = din("w_down", [DFF, D])
    colp_d = din("colp", [128, NCOLP])
    rowp_d = din("rowp", [8, D])
    lamv_d = din("lamv", [4, 64])
    ident_d = din("ident", [128, 128], BF16)
    masks_d = din("masks", [128, 256])
    kbrows_d = din("kbrows", [3, 128], BF16)
    qbrows_d = din("qbrows", [3, 4 * 512], BF16)
    btab_d = din("btab", [128, 256])
    hmask_d = din("hmask", [128, 1])
    y = nc.dram_tensor("y", [NOWN, D], F32, kind="ExternalOutput").ap()

    final_ops = []

    with ExitStack() as st:
        ARENA_BYTES = 206000
        U = st.enter_context(nc.sbuf_tensor("U", [128, ARENA_BYTES // 2], BF16))
        AR = Arena(U, ARENA_BYTES)
        PSt = [st.enter_context(nc.psum_tensor("PS%d" % i, [128, 1024], F32)) for i in range(4)]
        REG = []
        REGB = []
        for i in range(4):
            for hlf in range(2):
                REG.append(PSt[i][:, hlf * 512:(hlf + 1) * 512])
                REGB.append(Buf("bank%d" % (2 * i + hlf), excl=True))

        def bank_bf(bk):
            return REG[bk].bitcast(BF16)

        colp = AR.alloc([128, NCOLP], F32)
        ident = AR.alloc([128, 128], BF16)
        masks = AR.alloc([128, 256], F32)
        btab = AR.alloc([128, 256], F32)
        hmask = AR.alloc([128, 1], F32)
        lamv = AR.alloc([128, 4, 64], F32)
        sc = AR.alloc([128, 16], F32)
        lamt = AR.alloc([128, 2, 64], F32)
        ones_bf = AR.alloc([128, 128], BF16)
        onesm_bf = AR.alloc([128, 128], BF16)
        attT = AR.alloc([128, 4, NOWN], BF16)
        R = AR.alloc([128, 4, D], F32)
        zb = AR.alloc([128, 5, D], BF16)
        xT = AR.alloc([128, 8, 640], BF16)
        stt = AR.alloc([128, 5, 2, 6], F32)
        mv = AR.alloc([128, 5, 2], F32)
        lnt = AR.alloc([128, 5, 4], F32)
        b_colp, b_ident, b_masks, b_btab, b_hmask, b_lamv, b_sc = [Buf() for _ in range(7)]
        b_ones = Buf()
        b_attT = [Buf() for _ in range(4)]
        b_R = [Buf() for _ in range(4)]
        b_zb = [Buf() for _ in range(5)]
        b_xT = [Buf() for _ in range(8)]
        b_st = [Buf() for _ in range(5)]
        b_mv = Buf()
        b_lnt = Buf()
        pmark = AR.mark()

        def dma(eng, out, in_, key, reads=(), writes=()):
            return P.op(eng, lambda e: e.dma_start(out=out, in_=in_), reads=reads, writes=writes, dma_key=key)

        dma(SP, colp, colp_d, "c0", writes=[b_colp])
        dma(SP, ident, ident_d, "c1", writes=[b_ident])
        dma(SP, masks, masks_d, "c2", writes=[b_masks])
        dma(SP, btab, btab_d, "c3", writes=[b_btab])
        dma(SP, hmask, hmask_d, "c4", writes=[b_hmask])
        dma(SP, lamv, bass.AP(lamv_d.tensor, 0, [[0, 128], [64, 4], [1, 64]]), "c5", writes=[b_lamv])
        P.op(DVE, lambda e: e.memset(sc[:, 0:1], EPS), writes=[b_sc])
        P.op(DVE, lambda e: e.memset(ones_bf, 1.0), writes=[b_ones])
        P.op(DVE, lambda e: e.memset(onesm_bf, 1.0 / 128.0), writes=[b_ones])
        b_lamt = Buf()
        P.op(DVE, lambda e: e.tensor_tensor(out=lamt[:, 0, :], in0=lamv[:, 0, :], in1=lamv[:, 1, :], op=ALU.mult),
             reads=[b_lamv], writes=[b_lamt])
        P.op(DVE, lambda e: e.tensor_tensor(out=lamt[:, 1, :], in0=lamv[:, 2, :], in1=lamv[:, 3, :], op=ALU.mult),
             reads=[b_lamv], writes=[b_lamt])
        P.op(DVE, lambda e: e.reduce_sum(out=sc[:, 1:3], in_=lamt, axis=mybir.AxisListType.X),
             reads=[b_lamt], writes=[b_sc])
        P.op(ACT, lambda e: e.activation(out=sc[:, 3:5], in_=sc[:, 1:3], func=AF.Exp), reads=[b_sc], writes=[b_sc])
        P.op(DVE, lambda e: e.tensor_tensor(out=sc[:, 5:6], in0=sc[:, 4:5], in1=sc[:, 3:4], op=ALU.subtract),
             reads=[b_sc], writes=[b_sc])
        P.op(DVE, lambda e: e.tensor_scalar(out=sc[:, 5:6], in0=sc[:, 5:6], scalar1=-LAM_INIT, scalar2=None,
                                            op0=ALU.add), reads=[b_sc], writes=[b_sc])
        P.op(DVE, lambda e: e.tensor_scalar(out=sc[:, 6:7], in0=colp[:, C_SUB:C_SUB + 1], scalar1=1.0 - LAM_INIT,
                                            scalar2=None, op0=ALU.mult), reads=[b_colp, b_sc], writes=[b_sc])

        ckp('const')
        def load_x(src_rows, nblk, j0=0):
            dma(SP, R[:, j0:j0 + nblk, :], src_rows.rearrange("(j p) d -> p j d", p=128), "xR",
                writes=[b_R[j] for j in range(j0, j0 + nblk)])

        def ln_stats(srcs, nb, part="all"):
            srcs = [(a_, list(b_) if isinstance(b_, (tuple, list)) else [b_]) for a_, b_ in srcs]
            if part in ("all", "dve"):
              for j, (src, sb_) in enumerate(srcs):
                P.op(DVE, lambda e, j=j, src=src: e.bn_stats(out=stt[:, j, 0, :], in_=src[:, 0:512]),
                     reads=sb_, writes=[b_st[j]])
                P.op(DVE, lambda e, j=j, src=src: e.bn_stats(out=stt[:, j, 1, :], in_=src[:, 512:1024]),
                     reads=sb_, writes=[b_st[j]])
                P.op(DVE, lambda e, j=j: e.bn_aggr(out=mv[:, j, :], in_=stt[:, j, :, :]),
                     reads=[b_st[j]], writes=[b_mv])
            if part == "dve":
                return
            P.op(ACT, lambda e: e.activation(out=lnt[:, 0:nb, 0], in_=mv[:, 0:nb, 1], func=AF.Ln,
                                             bias=sc[:, 0:1], scale=1.0), reads=[b_mv, b_sc], writes=[b_lnt])
            P.op(ACT, lambda e: e.activation(out=lnt[:, 0:nb, 1], in_=lnt[:, 0:nb, 0], func=AF.Exp, scale=-0.5),
                 reads=[b_lnt], writes=[b_lnt])
            P.op(DVE, lambda e: e.scalar_tensor_tensor(out=lnt[:, 0:nb, 2], in0=mv[:, 0:nb, 0], scalar=-1.0,
                                                       in1=lnt[:, 0:nb, 1], op0=ALU.mult, op1=ALU.mult),
                 reads=[b_mv, b_lnt], writes=[b_lnt])

        def ln_apply_bf(srcs, zt=None, zbufs=None, eng=ACT):
            zt = zb if zt is None else zt
            zbufs = b_zb if zbufs is None else zbufs
            srcs = [(a_, list(b_) if isinstance(b_, (tuple, list)) else [b_]) for a_, b_ in srcs]
            for j, (src, sb_) in enumerate(srcs):
                if eng == DVE:
                    P.op(DVE, lambda e, j=j, src=src, zt=zt: e.tensor_scalar(
                        out=zt[:, j, :], in0=src, scalar1=lnt[:, j, 1:2], scalar2=lnt[:, j, 2:3],
                        op0=ALU.mult, op1=ALU.add), reads=sb_ + [b_lnt], writes=[zbufs[j]])
                    continue
                P.op(ACT, lambda e, j=j, src=src, zt=zt: e.activation(out=zt[:, j, :], in_=src, func=AF.Identity,
                                                                      bias=lnt[:, j, 2:3], scale=lnt[:, j, 1:2]),
                     reads=sb_ + [b_lnt], writes=[zbufs[j]])

        def ln_transposes(nb, gcol, bcol, tbanks, zt=None, zbufs=None, evac="mix"):
            zt = zb if zt is None else zt
            zbufs = b_zb if zbufs is None else zbufs
            for (j0, j1) in ((0, min(nb, 4)), (4, nb)):
                if j1 <= j0:
                    continue
                n = (j1 - j0) * 128
                c0 = j0 * 128
                for dt in range(8):
                    bk = tbanks[dt % 2]
                    tpb = bank_bf(bk)
                    tp = tpb[:, 0:n]
                    for j in range(j0, j1):
                        P.op(PE, lambda e, j=j, dt=dt, tpb=tpb, j0=j0, zt=zt: e.transpose(
                            out=tpb[:, (j - j0) * 128:(j - j0 + 1) * 128],
                            in_=zt[:, j, dt * 128:(dt + 1) * 128], identity=ident),
                            reads=[zbufs[j], b_ident], writes=[REGB[bk]])
                    if dt % 2 == 0 and evac == "mix":
                        P.op(DVE, lambda e, dt=dt, tp=tp, n=n, c0=c0: e.tensor_scalar(
                            out=xT[:, dt, c0:c0 + n], in0=tp, scalar1=colp[:, gcol + dt:gcol + dt + 1],
                            scalar2=colp[:, bcol + dt:bcol + dt + 1], op0=ALU.mult, op1=ALU.add),
                            reads=[REGB[bk], b_colp], writes=[b_xT[dt]])
                    else:
                        P.op(ACT, lambda e, dt=dt, tp=tp, n=n, c0=c0: e.activation(
                            out=xT[:, dt, c0:c0 + n], in_=tp, func=AF.Identity,
                            bias=colp[:, bcol + dt:bcol + dt + 1], scale=colp[:, gcol + dt:gcol + dt + 1]),
                            reads=[REGB[bk], b_colp], writes=[b_xT[dt]])

        def mm(out, lhsT, rhs, start, stop, reads, writes):
            P.op(PE, lambda e: e.matmul(out, lhsT=lhsT, rhs=rhs, start=start, stop=stop, skip_group_check=True),
                 reads=reads, writes=writes)

        KT = [[AR.alloc([67, SEQ], BF16) for c in range(2)] for hl in range(2)]
        Vt = AR.alloc([128, NBLK, 2, 129], BF16)
        QT = [[AR.alloc([67, 512], BF16) for c in range(2)] for hl in range(2)]
        zb2 = AR.alloc([128, 4, D], BF16)
        b_zb2 = [Buf() for _ in range(4)]
        wq = AR.alloc([128, 8, 256], BF16)
        wk = AR.alloc([128, 8, 256], BF16)
        wv = AR.alloc([128, 8, 256], BF16)
        Pt = [AR.alloc([128, 2, 512], BF16) for _ in range(2)]
        fa32 = AR.alloc([128, 128], F32)
        fatt = AR.alloc([128, 128], F32)
        fattb = AR.alloc([128, 128], BF16)
        frc = AR.alloc([128, 8], F32)
        fst = AR.alloc([128, 6], F32)
        fmv = AR.alloc([128, 4], F32)
        b_KT = [[[Buf() for _ in range(16)] for c in range(2)] for hl in range(2)]
        b_KTrow = Buf()
        b_V = [Buf() for _ in range(16)]
        b_Vone = Buf()
        b_QT = [[Buf() for c in range(2)] for hl in range(2)]
        b_QTrow = [[Buf() for c in range(2)] for hl in range(2)]
        b_wq, b_wk, b_wv = Buf(), Buf(), Buf()
        b_Pt = [Buf(), Buf()]
        b_fa, b_fatt, b_fattb, b_frc, b_fst, b_fmv = [Buf() for _ in range(6)]
        Oreg = []
        b_O = []
        for a_ in range(8):
            Oreg.append(REG[4 + a_ // 2][:, (a_ % 2) * 129:(a_ % 2) * 129 + 129])
            b_O.append(REGB[4 + a_ // 2])

        for hl in range(2):
            for c in range(2):
                dma(SP, KT[hl][c][64:67, :].rearrange("p (g k) -> p g k", k=128),
                    bass.AP(kbrows_d.tensor, 0, [[128, 3], [0, NBLK], [1, 128]]), "kr%d%d" % (hl, c),
                    writes=[b_KTrow])
        ckp('krows')
        P.op(DVE, lambda e: e.memset(Vt[:, :, :, 0:1], 1.0), writes=[b_Vone])
        ckp('rows')

        sp_idx = [0]
        pt_idx = [0]

        for hp in range(2):
            h0 = 2 * hp
            dma(POOL, wq, w_in[:, h0 * 128:h0 * 128 + 256].rearrange("(kt p) n -> p kt n", p=128), "wq",
                writes=[b_wq])
            dma(POOL, wk, w_in[:, 512 + h0 * 128:512 + h0 * 128 + 256].rearrange("(kt p) n -> p kt n", p=128),
                "wk", writes=[b_wk])
            dma(POOL, wv, w_in[:, 1024 + h0 * 128:1024 + h0 * 128 + 256].rearrange("(kt p) n -> p kt n", p=128),
                "wv", writes=[b_wv])
            for hl in range(2):
                for c in range(2):
                    dma(SP, QT[hl][c][64:67, :], qbrows_d[:, (h0 + hl) * 512:(h0 + hl + 1) * 512],
                        "qr%d%d" % (hl, c), writes=[b_QTrow[hl][c]])
            if hp == 0:
                ckp('wA')

            def finalize(hl, J, i):
                h = h0 + hl
                tb = 4 * J + i
                o1, o2 = Oreg[2 * i], Oreg[2 * i + 1]
                bo1, bo2 = b_O[2 * i], b_O[2 * i + 1]
                P.op(DVE, lambda e: e.reciprocal(out=frc[:, 0:1], in_=o1[:, 0:1]), reads=[bo1], writes=[b_frc])
                P.op(DVE, lambda e: e.reciprocal(out=frc[:, 1:2], in_=o2[:, 0:1]), reads=[bo2], writes=[b_frc])
                P.op(DVE, lambda e: e.tensor_tensor(out=frc[:, 2:3], in0=frc[:, 1:2], in1=sc[:, 5:6], op=ALU.mult),
                     reads=[b_frc, b_sc], writes=[b_frc])
                P.op(DVE, lambda e: e.tensor_scalar(out=fa32, in0=o1[:, 1:129], scalar1=frc[:, 0:1], scalar2=None,
                                                    op0=ALU.mult), reads=[bo1, b_frc], writes=[b_fa])
                P.op(DVE, lambda e: e.scalar_tensor_tensor(out=fatt, in0=o2[:, 1:129], scalar=frc[:, 2:3], in1=fa32,
                                                           op0=ALU.mult, op1=ALU.add),
                     reads=[bo2, b_frc, b_fa], writes=[b_fatt])
                P.op(DVE, lambda e: e.bn_stats(out=fst, in_=fatt), reads=[b_fatt], writes=[b_fst])
                P.op(DVE, lambda e: e.bn_aggr(out=fmv[:, 0:2], in_=fst), reads=[b_fst], writes=[b_fmv])
                P.op(DVE, lambda e: e.scalar_tensor_tensor(out=fmv[:, 2:3], in0=fmv[:, 0:1], scalar=fmv[:, 0:1],
                                                           in1=fmv[:, 1:2], op0=ALU.mult, op1=ALU.add),
                     reads=[b_fmv], writes=[b_fmv])
                P.op(ACT, lambda e: e.activation(out=fmv[:, 3:4], in_=fmv[:, 2:3], func=AF.Ln, bias=sc[:, 0:1],
                                                 scale=1.0), reads=[b_fmv, b_sc], writes=[b_fmv])
                P.op(ACT, lambda e: e.activation(out=fmv[:, 3:4], in_=fmv[:, 3:4], func=AF.Exp, scale=-0.5),
                     reads=[b_fmv], writes=[b_fmv])
                P.op(DVE, lambda e: e.tensor_scalar(out=fattb, in0=fatt, scalar1=fmv[:, 3:4], scalar2=None,
                                                    op0=ALU.mult), reads=[b_fatt, b_fmv], writes=[b_fattb])
                P.op(PE, lambda e: e.transpose(out=bank_bf(3)[:, 0:128], in_=fattb, identity=ident),
                     reads=[b_fattb, b_ident], writes=[REGB[3]])
                P.op(ACT, lambda e: e.activation(out=attT[:, h, tb * 128:(tb + 1) * 128], in_=bank_bf(3)[:, 0:128],
                                                 func=AF.Identity, scale=sc[:, 6:7]),
                     reads=[REGB[3], b_sc], writes=[b_attT[h]])

            jobs = []
            for J_ in range(NQT):
                jobs += [("kv", J_, 0), ("kv", J_, 1), ("q", J_, 0)]
            zsets = [(zb, b_zb), (zb2, b_zb2)]

            def job_src(job):
                kind, J_, half = job
                if kind == "kv":
                    g0_ = 8 * J_ + 4 * half
                    return xkv[g0_ * 128:(g0_ + 4) * 128, :]
                return xown[J_ * 512:(J_ + 1) * 512, :]

            def job_load(ji):
                load_x(job_src(jobs[ji]), 4)

            def job_stats(ji, part="all"):
                ln_stats([(R[:, j, :], b_R[j]) for j in range(4)], 4, part)

            def job_apply(ji, eng=ACT):
                zt, zbufs = zsets[ji % 2]
                ln_apply_bf([(R[:, j, :], b_R[j]) for j in range(4)], zt, zbufs, eng)

            def job_back(ji):
                kind, J_, half = jobs[ji]
                zt, zbufs = zsets[ji % 2]
                nxt = ji + 1 if ji + 1 < len(jobs) else None
                if nxt is not None:
                    job_load(nxt)
                    job_stats(nxt, "dve")
                ln_transposes(4, C_G0, C_B0, (2, 3), zt, zbufs, evac="act")
                if nxt is not None:
                    job_stats(nxt, "act")
                    job_apply(nxt, DVE)
                wsel, bsel = (wk, b_wk) if kind == "kv" else (wq, b_wq)
                grp_ = 2 * J_ + half
                g0_ = 8 * J_ + 4 * half
                ri = 0
                for hl in range(2):
                    for c in range(2):
                        r = 4 + ri % 4
                        ri += 1
                        co = (hl * 2 + c) * 64
                        for dt in range(8):
                            mm(REG[r][0:64, :], wsel[:, dt, co:co + 64], xT[:, dt, 0:512], dt == 0, dt == 7,
                               [bsel, b_xT[dt]], [REGB[r]])
                        if kind == "kv":
                            P.op(ACT, lambda e, hl=hl, c=c, r=r, g0_=g0_: e.activation(
                                out=KT[hl][c][0:64, g0_ * 128:(g0_ + 4) * 128], in_=REG[r][0:64, :], func=AF.Copy),
                                reads=[REGB[r]], writes=[b_KT[hl][c][grp_]])
                        else:
                            P.op(ACT, lambda e, hl=hl, c=c, r=r: e.activation(
                                out=QT[hl][c][0:64, :], in_=REG[r][0:64, :], func=AF.Identity, scale=0.125),
                                reads=[REGB[r]], writes=[b_QT[hl][c]])
                if kind == "kv":
                    for j in range(4):
                        r = j % 2
                        for dt in range(8):
                            mm(REG[r][:, 0:256], xT[:, dt, j * 128:(j + 1) * 128], wv[:, dt, :], dt == 0, dt == 7,
                               [b_wv, b_xT[dt]], [REGB[r]])
                        P.op(DVE, lambda e, r=r, g=g0_ + j: e.tensor_copy(
                            out=Vt[:, g, :, 1:129], in_=REG[r][:, 0:256].rearrange("p (a b) -> p a b", a=2)),
                            reads=[REGB[r], b_Vone], writes=[b_V[grp_]])

            job_load(0)
            job_stats(0)
            job_apply(0)
            for J in range(NQT):
                for jj in range(3):
                    job_back(3 * J + jj)
                for hl in range(2):
                    h = h0 + hl
                    for i in range(4):
                        P.op(DVE, lambda e, i=i: e.memset(REG[4 + i][:, 0:258], 0.0), writes=[REGB[4 + i]])
                    nunits = 8 * J + 8
                    spis = []

                    def unit_geo(g):
                        m = g - 8 * J
                        i0 = 0 if m < 0 else m // 2
                        return m, i0, i0 * 128

                    def emit_qk(g):
                        m, i0, q0 = unit_geo(g)
                        grp = g // 4
                        spi = sp_idx[0] % 2
                        sp_idx[0] += 1
                        spis.append(spi)
                        Sps = PSt[spi][:, :]
                        sb2 = [REGB[2 * spi], REGB[2 * spi + 1]]
                        for c in range(2):
                            mm(Sps[:, c * 512 + q0:(c + 1) * 512], KT[hl][c][0:67, g * 128:(g + 1) * 128],
                               QT[hl][c][0:67, q0:512], True, True,
                               [b_KT[hl][c][grp], b_KTrow, b_QT[hl][c], b_QTrow[hl][c]], [sb2[c]])

                    def emit_exp(g):
                        m, i0, q0 = unit_geo(g)
                        spi = spis[g]
                        Sps = PSt[spi][:, :]
                        sb2 = [REGB[2 * spi], REGB[2 * spi + 1]]
                        if m >= 0:
                            mo = (m % 2) * 128
                            for c in range(2):
                                P.op(DVE, lambda e, c=c, Sps=Sps, q0=q0, mo=mo: e.tensor_tensor(
                                    out=Sps[:, c * 512 + q0:c * 512 + q0 + 128],
                                    in0=Sps[:, c * 512 + q0:c * 512 + q0 + 128],
                                    in1=masks[:, mo:mo + 128], op=ALU.add),
                                    reads=[sb2[c], b_masks], writes=[sb2[c]])
                        pti = pt_idx[0] % 2
                        pt_idx[0] += 1
                        bcol = h * 64 + (m + BT_OFF)
                        P.op(ACT, lambda e, Sps=Sps, pti=pti, q0=q0, bcol=bcol: e.activation(
                            out=Pt[pti][:, :, q0:512],
                            in_=Sps.rearrange("p (a b) -> p a b", a=2)[:, :, q0:512],
                            func=AF.Exp, bias=btab[:, bcol:bcol + 1], scale=1.0),
                            reads=[sb2[0], sb2[1], b_btab], writes=[b_Pt[pti]])
                        return pti

                    def emit_pv(g, pti):
                        m, i0, q0 = unit_geo(g)
                        grp = g // 4
                        for i in range(i0, 4):
                            last = 8 * J + 2 * i + 1
                            for c in range(2):
                                a = 2 * i + c
                                mm(Oreg[a], Pt[pti][:, c, i * 128:(i + 1) * 128], Vt[:, g, hl, :],
                                   False, g == last, [b_Pt[pti], b_V[grp], b_Vone], [b_O[a]])
                        if m >= 0 and m % 2 == 1:
                            finalize(hl, J, m // 2)

                    emit_qk(0)
                    for g in range(nunits):
                        pti = emit_exp(g)
                        if g + 1 < nunits:
                            emit_qk(g + 1)
                        emit_pv(g, pti)
                    if J == 0 and hl == 0 and hp == 0:
                        ckp('att00')
                if J == 0 and hp == 0:
                    ckp('attJ0')
            if hp == 0:
                ckp('hp0')

        ckp('A')
        P.barrier()
        AR.reset(pmark)
        NSLOT = 4
        Wr = [AR.alloc([128, 8, 512], BF16) for _ in range(NSLOT)]
        b_W = [Buf() for _ in range(NSLOT)]
        hT = AR.alloc([128, NFT, 512], BF16)
        b_hT = [Buf() for _ in range(NFT)]
        qcT = hT[:, 0:8, :]
        caT = hT[:, 8:16, :]
        gbt = [AR.alloc([128, D], F32) for _ in range(2)]
        b_gbt = [Buf(), Buf()]
        R2 = AR.alloc([128, 4, D], F32)
        b_R2 = [Buf() for _ in range(4)]
        gluT = AR.alloc([128, 4, 4, 160], BF16)
        b_glu = [Buf() for _ in range(4)]
        tg = hT[:, 11:14, :].rearrange("p a b -> p (a b)").bitcast(F32)[:, 0:640]
        b_tg = Buf()
        _ct = hT[:, 14:22, :].rearrange("p a b -> p (a b)").bitcast(F32).rearrange("p (a b) -> p a b", a=4)
        y32 = _ct[:, 0, :]
        yb = AR.alloc([128, 512], BF16)
        ysq = AR.alloc([128, 512], BF16)
        m2 = _ct[:, 1, :]
        var = _ct[:, 2, :]
        dd = _ct[:, 3, :]
        b_y32, b_yb, b_ysq, b_m2, b_var, b_dd = [Buf() for _ in range(6)]
        siluT = AR.alloc([128, 4, 512], BF16)
        b_silu = [Buf() for _ in range(4)]
        cT = AR.alloc([128, 4, 512], BF16)
        b_cT = [Buf() for _ in range(4)]
        NDG = 8
        Dg = AR.alloc([128, NDG, 128], BF16)
        b_Dg = [Buf() for _ in range(NDG)]
        PTc = [AR.alloc([128, 2, 512], BF16) for _ in range(2)]
        b_PTc = [Buf(), Buf()]
        rinv = AR.alloc([128, 512], F32)
        b_rinv = Buf()
        sgx = AR.alloc([128, D], F32)
        sg = [sgx[:, 0:512], sgx[:, 512:1024]]
        b_sg = [Buf(), Buf()]
        xh32 = AR.alloc([128, D], F32)
        b_xh = Buf()
        kcT = AR.alloc([128, 8, MEM], BF16)
        vc = AR.alloc([128, 2, D], BF16)
        b_kcT, b_vc = Buf(), Buf()

        wslot = [0]
        gslot = [0]

        def load_slab(src, kt):
            s = wslot[0] % NSLOT
            wslot[0] += 1
            ncol = src.shape[1]
            dma(POOL, Wr[s][:, 0:kt, 0:ncol], src.rearrange("(kt p) n -> p kt n", p=128), "w%d" % s,
                writes=[b_W[s]])
            return s

        def load_row_bc(row):
            s = gslot[0] % 2
            gslot[0] += 1
            dma(SP, gbt[s], bass.AP(rowp_d.tensor, row * D, [[0, 128], [1, D]]), "g%d" % s, writes=[b_gbt[s]])
            return s

        memT = hT[:, 0:4, :].rearrange("p a b -> p (a b)").rearrange("p (a b) -> p a b", a=8)
        b_memT = Buf()
        dma(SP, R[:, 0:2, :], memd.rearrange("(j p) d -> p j d", p=128), "xR", writes=[b_R[0], b_R[1]])
        for j in range(2):
            P.op(ACT, lambda e, j=j, R=R: e.activation(out=zb[:, j, :], in_=R[:, j, :], func=AF.Copy),
                 reads=[b_R[j]], writes=[b_zb[j]])
        for dt in range(8):
            bk = 6 + dt % 2
            for j in range(2):
                P.op(PE, lambda e, j=j, dt=dt, bk=bk: e.transpose(
                    out=bank_bf(bk)[:, j * 128:(j + 1) * 128],
                    in_=zb[:, j, dt * 128:(dt + 1) * 128], identity=ident),
                    reads=[b_zb[j], b_ident], writes=[REGB[bk]])
            P.op(DVE, lambda e, dt=dt, bk=bk: e.tensor_copy(out=memT[:, dt, :], in_=bank_bf(bk)[:, 0:256]),
                 reads=[REGB[bk]], writes=[b_memT])
        for q4 in range(2):
            s = load_slab(w_kv_mem[:, q4 * 512:(q4 + 1) * 512], 8)
            for nn in range(4):
                nt = q4 * 4 + nn
                r = nt % 3
                for dt in range(8):
                    mm(REG[r][:, 0:MEM], Wr[s][:, dt, nn * 128:(nn + 1) * 128], memT[:, dt, :], dt == 0, dt == 7,
                       [b_W[s], b_memT], [REGB[r]])
                P.op(ACT, lambda e, r=r, nt=nt: e.activation(out=kcT[:, nt, :], in_=REG[r][:, 0:MEM], func=AF.Copy),
                     reads=[REGB[r]], writes=[b_kcT])
        for q4 in range(2):
            s = load_slab(w_kv_mem[:, D + q4 * 512:D + (q4 + 1) * 512], 8)
            for mt in range(2):
                r = 3 + mt
                for dt in range(8):
                    mm(REG[r], memT[:, dt, mt * 128:(mt + 1) * 128], Wr[s][:, dt, :], dt == 0, dt == 7,
                       [b_W[s], b_memT], [REGB[r]])
                P.op(DVE, lambda e, r=r, mt=mt, q4=q4: e.tensor_copy(out=vc[:, mt, q4 * 512:(q4 + 1) * 512],
                                                                      in_=REG[r]),
                     reads=[REGB[r]], writes=[b_vc])

        ckp('mem')
        P.barrier()

        def residual_mm(li, lhs_tiles, lhs_bufs, wsrc, nk):
            nslab = (nk + 7) // 8
            for nh in range(2):
                slabs = []
                for s_ in range(nslab):
                    k0 = s_ * 8
                    kt = min(8, nk - k0)
                    slabs.append((load_slab(wsrc[k0 * 128:(k0 + kt) * 128, nh * 512:(nh + 1) * 512], kt), k0, kt))
                for si, (s, k0, kt) in enumerate(slabs):
                    for tb in range(4):
                        r = tb
                        for kk in range(kt):
                            f = k0 + kk
                            mm(REG[r], lhs_tiles(f)[:, tb * 128:(tb + 1) * 128], Wr[s][:, kk, :],
                               f == 0, f == nk - 1, [lhs_bufs[f], b_W[s]], [REGB[r]])
                        if si == nslab - 1:
                            P.op(DVE, lambda e, tb=tb, r=r, nh=nh, R=R: e.scalar_tensor_tensor(
                                out=R[:, tb, nh * 512:(nh + 1) * 512], in0=R[:, tb, nh * 512:(nh + 1) * 512],
                                scalar=ALPHA, in1=REG[r], op0=ALU.mult, op1=ALU.add),
                                reads=[b_R[tb], REGB[r]], writes=[b_R[tb]])

        def x_z(nb=4):
            for j in range(nb):
                P.op(DVE, lambda e, j=j, R=R: e.tensor_scalar(out=R[:, j, :], in0=R[:, j, :], scalar1=lnt[:, j, 1:2],
                                                              scalar2=lnt[:, j, 2:3], op0=ALU.mult, op1=ALU.add),
                     reads=[b_lnt], writes=[b_R[j]])

        def x_affine(grow, brow):
            sg_ = load_row_bc(grow)
            sb_ = load_row_bc(brow)
            for j in range(4):
                P.op(DVE, lambda e, j=j, sg_=sg_, R=R: e.tensor_tensor(out=R[:, j, :], in0=R[:, j, :], in1=gbt[sg_],
                                                                       op=ALU.mult),
                     reads=[b_gbt[sg_]], writes=[b_R[j]])
                P.op(DVE, lambda e, j=j, sb_=sb_, R=R: e.tensor_tensor(out=R[:, j, :], in0=R[:, j, :], in1=gbt[sb_],
                                                                       op=ALU.add),
                     reads=[b_gbt[sb_]], writes=[b_R[j]])

        def ln_full(gcol, bcol, grow, brow):
            srcs = [(R[:, j, :], b_R[j]) for j in range(4)]
            ln_stats(srcs, 4)
            ln_apply_bf(srcs)
            ln_transposes(4, gcol, bcol, (6, 7))
            x_z()
            x_affine(grow, brow)

        def prologue_load(ck_):
            load_x(xown[ck_ * 512:(ck_ + 1) * 512, :], 4)
            dma(SP, xh32, xhalo[ck_ * 128:(ck_ + 1) * 128, :], "xh", writes=[b_xh])

        def prologue_a(ck_):
            srcs = [(R[:, j, :], b_R[j]) for j in range(4)] + [(xh32, b_xh)]
            ln_stats(srcs, 5)
            ln_apply_bf(srcs)

        def prologue_b(ck_):
            ln_transposes(5, C_G0, C_B0, (6, 7))
            x_z()
            x_affine(0, 1)

        def epilogue(ck_):
            srcs = [(R[:, j, :], b_R[j]) for j in range(4)]
            ln_stats(srcs, 4)
            x_z()
            x_affine(6, 7)
            final_ops.append(dma(SP, y[ck_ * 512:(ck_ + 1) * 512, :].rearrange("(j p) d -> p j d", p=128), R, "out",
                                 reads=b_R))

        dg_idx = [0]

        Rs = [(R, b_R), (R2, b_R2)]
        prologue_load(0)
        prologue_a(0)
        prologue_b(0)
        for ck in range(NQT):
            R, b_R = Rs[ck % 2]
            ckp('ln0')
            s_val = load_slab(w_in[:, 1536:2048], 8)
            s_gate = load_slab(w_in[:, 2048:2560], 8)
            s_pw = load_slab(w_pw, 4)
            for ct in range(4):
                pv, pg = PSt[0], PSt[1]
                for (ps_, s_, rb) in ((pv, s_val, (REGB[0], REGB[1])), (pg, s_gate, (REGB[2], REGB[3]))):
                    for dt in range(8):
                        mm(ps_[:, 0:512], Wr[s_][:, dt, ct * 128:(ct + 1) * 128], xT[:, dt, 0:512], dt == 0, dt == 7,
                           [b_W[s_], b_xT[dt]], [rb[0]])
                    for dt in range(8):
                        mm(ps_[:, 512:640], Wr[s_][:, dt, ct * 128:(ct + 1) * 128], xT[:, dt, 512:640], dt == 0,
                           dt == 7, [b_W[s_], b_xT[dt]], [rb[1]])
                P.op(ACT, lambda e: e.activation(out=tg, in_=PSt[1][:, 0:640], func=AF.Tanh, scale=0.5),
                     reads=[REGB[2], REGB[3]], writes=[b_tg])
                P.op(DVE, lambda e, ct=ct: e.scalar_tensor_tensor(
                    out=gluT[:, ct, :, 32:160], in0=tg[:, 0:512].rearrange("p (a b) -> p a b", a=4), scalar=1.0,
                    in1=PSt[0][:, 0:512].rearrange("p (a b) -> p a b", a=4), op0=ALU.add, op1=ALU.mult),
                    reads=[b_tg, REGB[0]], writes=[b_glu[ct]])
                P.op(DVE, lambda e, ct=ct: e.scalar_tensor_tensor(
                    out=gluT[:, ct, :, 0:32], in0=tg[:, 512:640].rearrange("p (a b) -> p a b", a=4), scalar=1.0,
                    in1=PSt[0][:, 512:640].rearrange("p (a b) -> p a b", a=4), op0=ALU.add, op1=ALU.mult),
                    reads=[b_tg, REGB[1]], writes=[b_glu[ct]])
                if ck == 0:
                    P.op(DVE, lambda e, ct=ct: e.tensor_scalar(out=gluT[:, ct, 0, 0:32], in0=gluT[:, ct, 0, 0:32],
                                                               scalar1=hmask[:, 0:1], scalar2=None, op0=ALU.mult),
                         reads=[b_hmask], writes=[b_glu[ct]])
            ckp('glu')
            for ct in range(4):
                for k in range(31):
                    di = dg_idx[0] % NDG
                    dg_idx[0] += 1
                    wc = C_CW + ct * 31 + k
                    P.op(DVE, lambda e, di=di, wc=wc: e.tensor_scalar(out=Dg[:, di, :], in0=ident,
                                                                       scalar1=colp[:, wc:wc + 1], scalar2=None,
                                                                       op0=ALU.mult),
                         reads=[b_ident, b_colp], writes=[b_Dg[di]])
                    mm(REG[4].rearrange("p (a b) -> p a b", a=4), Dg[:, di, :], gluT[:, ct, :, 2 + k:2 + k + 128],
                       k == 0, k == 30, [b_Dg[di], b_glu[ct]], [REGB[4]])
                P.op(ACT, lambda e, ct=ct: e.activation(out=y32, in_=REG[4], func=AF.Identity,
                                                        bias=colp[:, C_CB + ct:C_CB + ct + 1], scale=0.5),
                     reads=[REGB[4], b_colp], writes=[b_y32])
                P.op(DVE, lambda e: e.tensor_copy(out=yb, in_=y32), reads=[b_y32], writes=[b_yb])
                P.op(ACT, lambda e: e.activation(out=ysq, in_=y32, func=AF.Square), reads=[b_y32], writes=[b_ysq])
                mm(REG[5], onesm_bf, yb, True, True, [b_ones, b_yb], [REGB[5]])
                mm(REG[6], onesm_bf, ysq, True, True, [b_ones, b_ysq], [REGB[6]])
                P.op(ACT, lambda e: e.activation(out=m2, in_=REG[5], func=AF.Square), reads=[REGB[5]], writes=[b_m2])
                P.op(DVE, lambda e: e.tensor_tensor(out=var, in0=REG[6], in1=m2, op=ALU.subtract),
                     reads=[REGB[6], b_m2], writes=[b_var])
                P.op(ACT, lambda e: e.activation(out=var, in_=var, func=AF.Ln, bias=sc[:, 0:1], scale=1.0),
                     reads=[b_sc], writes=[b_var])
                P.op(ACT, lambda e: e.activation(out=var, in_=var, func=AF.Exp, scale=-0.5), writes=[b_var])
                P.op(DVE, lambda e: e.tensor_tensor(out=dd, in0=y32, in1=REG[5], op=ALU.subtract),
                     reads=[b_y32, REGB[5]], writes=[b_dd])
                P.op(DVE, lambda e: e.tensor_tensor(out=dd, in0=dd, in1=var, op=ALU.mult),
                     reads=[b_var], writes=[b_dd])
                P.op(ACT, lambda e, ct=ct: e.activation(out=siluT[:, ct, :], in_=dd, func=AF.Silu,
                                                        bias=colp[:, C_CNB + ct:C_CNB + ct + 1],
                                                        scale=colp[:, C_CNG + ct:C_CNG + ct + 1]),
                     reads=[b_dd, b_colp], writes=[b_silu[ct]])
            for nt in range(4):
                r = nt % 4
                for ct in range(4):
                    mm(REG[r], Wr[s_pw][:, ct, nt * 128:(nt + 1) * 128], siluT[:, ct, :], ct == 0, ct == 3,
                       [b_W[s_pw], b_silu[ct]], [REGB[r]])
                P.op(ACT, lambda e, nt=nt, r=r: e.activation(out=cT[:, nt, :], in_=REG[r], func=AF.Identity,
                                                             bias=colp[:, C_BPW + nt:C_BPW + nt + 1], scale=1.0),
                     reads=[REGB[r], b_colp], writes=[b_cT[nt]])
            ckp('conv')
            residual_mm(0, lambda f: (attT[:, f, ck * 512:(ck + 1) * 512] if f < 4 else cT[:, f - 4, :]),
                        [b_attT[0], b_attT[1], b_attT[2], b_attT[3]] + b_cT, w_o, 8)
            ln_full(C_G1, C_B1, 2, 3)
            ckp('mix')
            for q4 in range(2):
                s = load_slab(w_q_mem[:, q4 * 512:(q4 + 1) * 512], 8)
                for nn in range(4):
                    nt = q4 * 4 + nn
                    r = 4 + nt % 3
                    for dt in range(8):
                        mm(REG[r], Wr[s][:, dt, nn * 128:(nn + 1) * 128], xT[:, dt, 0:512], dt == 0, dt == 7,
                           [b_W[s], b_xT[dt]], [REGB[r]])
                    P.op(ACT, lambda e, nt=nt, r=r: e.activation(out=qcT[:, nt, :], in_=REG[r], func=AF.Identity,
                                                                 scale=1.0 / 16.0),
                         reads=[REGB[r]], writes=[b_hT[nt]])
            for hh in range(4):
                pi = hh % 2
                for mt in range(2):
                    r = mt
                    for e2 in range(2):
                        nt = 2 * hh + e2
                        mm(REG[r], kcT[:, nt, mt * 128:(mt + 1) * 128], qcT[:, nt, :], e2 == 0, e2 == 1,
                           [b_kcT, b_hT[nt]], [REGB[r]])
                    P.op(ACT, lambda e, pi=pi, mt=mt, r=r: e.activation(out=PTc[pi][:, mt, :], in_=REG[r],
                                                                         func=AF.Exp),
                         reads=[REGB[r]], writes=[b_PTc[pi]])
                for mt in range(2):
                    mm(REG[4], ones_bf, PTc[pi][:, mt, :], mt == 0, mt == 1, [b_ones, b_PTc[pi]], [REGB[4]])
                P.op(DVE, lambda e: e.reciprocal(out=rinv, in_=REG[4]), reads=[REGB[4]], writes=[b_rinv])
                for e2 in range(2):
                    nt = 2 * hh + e2
                    r = 2 + e2
                    for mt in range(2):
                        mm(REG[r], vc[:, mt, nt * 128:(nt + 1) * 128], PTc[pi][:, mt, :], mt == 0, mt == 1,
                           [b_vc, b_PTc[pi]], [REGB[r]])
                    P.op(DVE, lambda e, nt=nt, r=r: e.tensor_tensor(out=caT[:, nt, :], in0=REG[r], in1=rinv,
                                                                    op=ALU.mult),
                         reads=[REGB[r], b_rinv], writes=[b_hT[8 + nt]])
            residual_mm(1, lambda f: caT[:, f, :], [b_hT[8 + f] for f in range(8)], w_o_mem, 8)
            ln_full(C_G2, C_B2, 4, 5)
            ckp('ca')
            cap_e, cap_p = [], []
            if ck > 0:
                R, b_R = Rs[(ck - 1) % 2]
                P.capture_begin()
                epilogue(ck - 1)
                cap_e = P.capture_end()
                R, b_R = Rs[ck % 2]
            ft = 0
            for fg in range(6):
                ncol = 512 if fg < 5 else 256
                sg_s = load_slab(w_gate[:, fg * 512:fg * 512 + ncol], 8)
                su_s = load_slab(w_up[:, fg * 512:fg * 512 + ncol], 8)
                for nn in range(ncol // 128):
                    rg = 2 * (ft % 3)
                    ru = rg + 1
                    for dt in range(8):
                        mm(REG[rg], Wr[sg_s][:, dt, nn * 128:(nn + 1) * 128], xT[:, dt, 0:512], dt == 0, dt == 7,
                           [b_W[sg_s], b_xT[dt]], [REGB[rg]])
                    for dt in range(8):
                        mm(REG[ru], Wr[su_s][:, dt, nn * 128:(nn + 1) * 128], xT[:, dt, 0:512], dt == 0, dt == 7,
                           [b_W[su_s], b_xT[dt]], [REGB[ru]])
                    si = ft % 2
                    P.op(ACT, lambda e, si=si, rg=rg: e.activation(out=sg[si], in_=REG[rg], func=AF.Silu),
                         reads=[REGB[rg]], writes=[b_sg[si]])
                    P.op(DVE, lambda e, si=si, ru=ru, ft=ft: e.tensor_tensor(out=hT[:, ft, :], in0=sg[si],
                                                                               in1=REG[ru], op=ALU.mult),
                         reads=[b_sg[si], REGB[ru]], writes=[b_hT[ft]])
                    ft += 1
                    P.replay(cap_e, 3)
                    if ft == 11 and ck + 1 < NQT:
                        P.replay(cap_e, 1000)
                        R, b_R = Rs[(ck + 1) % 2]
                        prologue_load(ck + 1)
                        P.capture_begin()
                        prologue_a(ck + 1)
                        cap_p = P.capture_end()
                        R, b_R = Rs[ck % 2]
                    if ft > 11:
                        P.replay(cap_p, 3)
            P.replay(cap_e, 1000)
            P.replay(cap_p, 1000)
            if ck + 1 < NQT:
                R, b_R = Rs[(ck + 1) % 2]
                prologue_b(ck + 1)
                R, b_R = Rs[ck % 2]
            residual_mm(2, lambda f: hT[:, f, :], b_hT, w_down, NFT)
            ckp('ffn')
            ckp('out0')

        R, b_R = Rs[(NQT - 1) % 2]
        epilogue(NQT - 1)
        P.emit(final_ops=final_ops[-1:])
    return nc


_CACHE = {}


def _consts(parity):
    bf = ml_dtypes.bfloat16
    ident = np.eye(128, dtype=np.float32).astype(bf)
    kp = np.arange(128)[:, None]
    qp = np.arange(128)[None, :]
    diag = np.where(kp <= qp, 0.0, NEG).astype(np.float32)
    full = np.full((128, 128), NEG, np.float32)
    zero = np.zeros((128, 128), np.float32)
    if parity == 0:
        masks = np.concatenate([diag, full], axis=1)
    else:
        masks = np.concatenate([zero, diag], axis=1)
    kbrows = np.stack([np.arange(128, dtype=np.float32), np.ones(128, np.float32), np.ones(128, np.float32)]).astype(bf)
    qb = np.zeros((3, 4, 512), np.float32)
    col = np.arange(512)
    for h in range(4):
        qb[0, h, :] = SLOPES[h]
        qb[1, h, :] = -SLOPES[h] * (col % 128)
        qb[2, h, :] = -SLOPES[h] * 256.0 * (col // 128)
    qbrows = qb.reshape(3, 2048).astype(bf)
    bt = np.zeros((128, 256), np.float32)
    for h in range(4):
        for dd_ in range(-BT_OFF, 8):
            bt[:, h * 64 + dd_ + BT_OFF] = SLOPES[h] * 128.0 * dd_
    hmask = np.full((128, 1), 0.0 if parity == 0 else 1.0, np.float32)
    return dict(ident=ident, masks=masks, kbrows=kbrows, qbrows=qbrows, btab=bt, hmask=hmask)


def kernel(x, mem, in_norm_g, in_norm_b, w_in, lambda_q1, lambda_k1, lambda_q2, lambda_k2,
           subln_g, conv_w, conv_b, conv_norm_g, conv_norm_b, w_pw, b_pw, w_o, ln1_g, ln1_b,
           w_q_mem, w_kv_mem, w_o_mem, ln2_g, ln2_b, w_gate, w_up, w_down, ln3_g, ln3_b):
    f32 = np.float32
    x = np.asarray(x, f32)
    mem = np.asarray(mem, f32)

    def colv(v, n):
        return np.asarray(v, f32).reshape(n, 128).T

    colp = np.zeros((128, NCOLP), f32)
    colp[:, C_G0:C_G0 + 8] = colv(in_norm_g, 8)
    colp[:, C_B0:C_B0 + 8] = colv(in_norm_b, 8)
    colp[:, C_G1:C_G1 + 8] = colv(ln1_g[0], 8)
    colp[:, C_B1:C_B1 + 8] = colv(ln1_b[0], 8)
    colp[:, C_G2:C_G2 + 8] = colv(ln2_g[0], 8)
    colp[:, C_B2:C_B2 + 8] = colv(ln2_b[0], 8)
    colp[:, C_CB:C_CB + 4] = colv(conv_b[0], 4)
    colp[:, C_CNG:C_CNG + 4] = colv(conv_norm_g[0], 4)
    colp[:, C_CNB:C_CNB + 4] = colv(conv_norm_b[0], 4)
    colp[:, C_BPW:C_BPW + 4] = colv(b_pw[0], 4)
    colp[:, C_SUB] = np.asarray(subln_g[0], f32)
    cw = np.asarray(conv_w[0], f32)
    colp[:, C_CW:C_CW + 124] = cw.reshape(31, 4, 128).transpose(2, 1, 0).reshape(128, 124)
    rowp = np.stack([np.asarray(v, f32).reshape(D) for v in
                     (in_norm_g, in_norm_b, ln1_g[0], ln1_b[0], ln2_g[0], ln2_b[0], ln3_g[0], ln3_b[0])])
    lamv = np.stack([np.asarray(v, f32).reshape(64) for v in (lambda_q1, lambda_k1, lambda_q2, lambda_k2)])

    shared = dict(
        w_in=np.ascontiguousarray(np.asarray(w_in, f32)[0]), w_pw=np.ascontiguousarray(np.asarray(w_pw, f32)[0]),
        w_o=np.ascontiguousarray(np.asarray(w_o, f32)[0]), w_q_mem=np.ascontiguousarray(np.asarray(w_q_mem, f32)[0]),
        w_kv_mem=np.ascontiguousarray(np.asarray(w_kv_mem, f32)[0]),
        w_o_mem=np.ascontiguousarray(np.asarray(w_o_mem, f32)[0]),
        w_gate=np.ascontiguousarray(np.asarray(w_gate, f32)[0]), w_up=np.ascontiguousarray(np.asarray(w_up, f32)[0]),
        w_down=np.ascontiguousarray(np.asarray(w_down, f32)[0]),
        colp=colp, rowp=np.ascontiguousarray(rowp), lamv=np.ascontiguousarray(lamv))

    in_maps = []
    for core in range(8):
        b, par = core // 2, core % 2
        xb = x[b].reshape(NBLK, 128, D)
        own = np.ascontiguousarray(xb[par::2].reshape(NOWN, D))
        halo = np.zeros((32, 32, D), f32)
        for j in range(32):
            g = 2 * j + par
            if g > 0:
                halo[j] = xb[g - 1, 96:128, :]
        m = dict(shared)
        m.update(xkv=np.ascontiguousarray(x[b]), xown=own, xhalo=halo.reshape(1024, D),
                 mem=np.ascontiguousarray(mem[b]))
        m.update(_consts(par))
        in_maps.append(m)

    if "nc" not in _CACHE:
        _CACHE["nc"] = build_program()
    nc = _CACHE["nc"]
    res = run_bass_kernel_spmd(nc, in_maps, core_ids=list(range(8)))
    out = np.zeros((4, SEQ, D), f32)
    ov = out.reshape(4, NBLK, 128, D)
    for core in range(8):
        b, par = core // 2, core % 2
        ov[b, par::2] = np.asarray(res.results[core]["y"], f32).reshape(32, 128, D)
    return out
```
